# Optimizing a Trainium2 kernel written in Bass

```python
import math
import jax, jax.numpy as jnp
from jax import lax
import numpy as np

D_MODEL = 1024
BATCH = 32
SEQ = 2048
DEPTH = 1

CHUNK = 64
Q_BLOCK = 128
HEAD_DIM = 64
N_HEADS_DSA = 8
N_HEADS_SB = 8
IDX_HEADS = 8
IDX_DIM = 64
TOPK_MAX = 256
N_REL_BUCKETS = 32
REL_MAX_DIST = 128
D_FF = 2816
CONV_WIDTH = 3
LN_EPS = 1e-5
DEEPNORM_ALPHA = (2.0 * DEPTH) ** 0.25
DEEPNORM_BETA = (8.0 * DEPTH) ** -0.25
WIDTH_DSA = N_HEADS_DSA * HEAD_DIM
WIDTH_SB = N_HEADS_SB * HEAD_DIM
IN_SIZES = (WIDTH_DSA, WIDTH_DSA, WIDTH_DSA, IDX_HEADS * IDX_DIM, IDX_DIM, IDX_HEADS,
            WIDTH_SB, WIDTH_SB, WIDTH_SB, D_MODEL, D_MODEL)
N_IN = sum(IN_SIZES)
IN_OFFSETS = tuple(int(o) for o in np.cumsum(IN_SIZES)[:-1])

kernel_name = 'hybrid_dsa_stickbreak_convffn_deepnorm'


def layer_norm(x, g, b):
    xf = x.astype(jnp.float32)
    mu = jnp.mean(xf, axis=-1, keepdims=True)
    var = jnp.mean(jnp.square(xf - mu), axis=-1, keepdims=True)
    return ((xf - mu) * lax.rsqrt(var + LN_EPS)).astype(x.dtype) * g + b


def t5_bucket(rel):
    nb = N_REL_BUCKETS // 2
    max_exact = nb // 2
    side = jnp.where(rel > 0, nb, 0)
    n = jnp.abs(rel)
    nf = jnp.maximum(n, 1).astype(jnp.float32)
    large = max_exact + (jnp.log(nf / max_exact) / math.log(REL_MAX_DIST / max_exact)
                         * (nb - max_exact)).astype(jnp.int32)
    large = jnp.minimum(large, nb - 1)
    return side + jnp.where(n < max_exact, n, large)


def dsa_sequence(q, k, v, q_idx, k_idx, w_idx, rel_bias):
    S = q.shape[0]
    n_sel = min(TOPK_MAX, S // 4)
    nb = S // Q_BLOCK
    key_chunk = jnp.arange(S) // CHUNK

    def block(args):
        qb, qib, wib, t0 = args
        t = t0 + jnp.arange(Q_BLOCK)
        q_chunk = (t // CHUNK)[:, None]
        dots = jnp.einsum('thd,sd->ths', qib, k_idx).astype(jnp.float32) * IDX_DIM ** -0.5
        score = jnp.einsum('th,ths->ts', wib.astype(jnp.float32), jax.nn.relu(dots))
        score = jnp.where(key_chunk[None, :] <= q_chunk, score, -jnp.inf)
        _, sel = lax.top_k(score, n_sel)
        valid = (sel // CHUNK) <= q_chunk
        kg = k[sel]
        vg = v[sel]
        logits = jnp.einsum('thd,tnhd->htn', qb, kg).astype(jnp.float32) * HEAD_DIM ** -0.5
        bias = rel_bias[t5_bucket(sel - t[:, None])]
        logits = logits + jnp.transpose(bias, (2, 0, 1)).astype(jnp.float32)
        logits = jnp.where(valid[None], logits, -jnp.inf)
        p = jax.nn.softmax(logits, axis=-1).astype(v.dtype)
        return jnp.einsum('htn,tnhd->thd', p, vg)

    blocks = (q.reshape(nb, Q_BLOCK, N_HEADS_DSA, HEAD_DIM),
              q_idx.reshape(nb, Q_BLOCK, IDX_HEADS, IDX_DIM),
              w_idx.reshape(nb, Q_BLOCK, IDX_HEADS),
              jnp.arange(nb) * Q_BLOCK)
    out = lax.map(block, blocks)
    return out.reshape(S, WIDTH_DSA)


def sb_sequence(q, k, v):
    S = q.shape[0]
    nb = S // Q_BLOCK
    s_pos = jnp.arange(S)

    def block(args):
        qb, t0 = args
        t = t0 + jnp.arange(Q_BLOCK)
        causal = s_pos[None, :] < t[:, None]
        z = jnp.einsum('thd,shd->hts', qb, k).astype(jnp.float32) * HEAD_DIM ** -0.5
        log_beta = jax.nn.log_sigmoid(z)
        log_keep = jnp.where(causal, jax.nn.log_sigmoid(-z), 0.0)
        later = lax.cumsum(log_keep, axis=2, reverse=True) - log_keep
        a = jnp.where(causal, jnp.exp(log_beta + later), 0.0).astype(v.dtype)
        return jnp.einsum('hts,shd->thd', a, v)

    out = lax.map(block, (q.reshape(nb, Q_BLOCK, N_HEADS_SB, HEAD_DIM), jnp.arange(nb) * Q_BLOCK))
    return out.reshape(S, WIDTH_SB)


def causal_dwconv(u, w, b):
    out = lax.conv_general_dilated(u, w[:, None, :], window_strides=(1,),
                                   padding=[(CONV_WIDTH - 1, 0)],
                                   dimension_numbers=('NWC', 'WIO', 'NWC'),
                                   feature_group_count=u.shape[-1])
    return out + b


def setup_inputs(seed: int = 0) -> dict:
    key = jax.random.key(seed)
    ks = jax.random.split(key, 16)
    f32 = jnp.float32
    nrm = lambda k, shape, scale: jax.random.normal(k, shape, f32) * scale
    return {
        'x': nrm(ks[0], (BATCH, SEQ, D_MODEL), 1.0),
        'rel_bias': nrm(ks[1], (N_REL_BUCKETS, N_HEADS_DSA), 0.5),
        'w_in': nrm(ks[2], (DEPTH, D_MODEL, N_IN), D_MODEL ** -0.5),
        'b_gates': nrm(ks[3], (DEPTH, 2, D_MODEL), 0.02),
        'w_branch_dsa': nrm(ks[4], (DEPTH, WIDTH_DSA, D_MODEL), WIDTH_DSA ** -0.5),
        'w_branch_sb': nrm(ks[5], (DEPTH, WIDTH_SB, D_MODEL), WIDTH_SB ** -0.5),
        'w_out': nrm(ks[6], (DEPTH, D_MODEL, D_MODEL), DEEPNORM_BETA * D_MODEL ** -0.5),
        'ln1_g': 1.0 + nrm(ks[7], (DEPTH, D_MODEL), 0.02),
        'ln1_b': nrm(ks[8], (DEPTH, D_MODEL), 0.02),
        'w_ffn_in': nrm(ks[9], (DEPTH, D_MODEL, 2 * D_FF), D_MODEL ** -0.5),
        'conv_w': nrm(ks[10], (DEPTH, CONV_WIDTH, D_FF), CONV_WIDTH ** -0.5),
        'conv_b': nrm(ks[11], (DEPTH, D_FF), 0.02),
        'w_ffn_out': nrm(ks[12], (DEPTH, D_FF, D_MODEL), DEEPNORM_BETA * D_FF ** -0.5),
        'ln2_g': 1.0 + nrm(ks[13], (DEPTH, D_MODEL), 0.02),
        'ln2_b': nrm(ks[14], (DEPTH, D_MODEL), 0.02),
    }


def reference(x, rel_bias, w_in, b_gates, w_branch_dsa, w_branch_sb, w_out, ln1_g, ln1_b,
              w_ffn_in, conv_w, conv_b, w_ffn_out, ln2_g, ln2_b):
    B, S, _ = x.shape
    for layer in range(DEPTH):
        proj = x @ w_in[layer]
        q_a, k_a, v_a, q_i, k_i, w_i, q_s, k_s, v_s, g_a, g_b = jnp.split(proj, IN_OFFSETS, axis=-1)
        w_i = w_i * IDX_HEADS ** -0.5
        y_a = lax.map(lambda a: dsa_sequence(*a, rel_bias),
                      (q_a.reshape(B, S, N_HEADS_DSA, HEAD_DIM),
                       k_a.reshape(B, S, N_HEADS_DSA, HEAD_DIM),
                       v_a.reshape(B, S, N_HEADS_DSA, HEAD_DIM),
                       q_i.reshape(B, S, IDX_HEADS, IDX_DIM), k_i, w_i))
        y_s = lax.map(lambda a: sb_sequence(*a),
                      (q_s.reshape(B, S, N_HEADS_SB, HEAD_DIM),
                       k_s.reshape(B, S, N_HEADS_SB, HEAD_DIM),
                       v_s.reshape(B, S, N_HEADS_SB, HEAD_DIM)))
        gate_a = jax.nn.sigmoid(g_a + b_gates[layer, 0])
        gate_b = jax.nn.sigmoid(g_b + b_gates[layer, 1])
        merged = gate_a * (y_a @ w_branch_dsa[layer]) + gate_b * (y_s @ w_branch_sb[layer])
        x = layer_norm(DEEPNORM_ALPHA * x + merged @ w_out[layer], ln1_g[layer], ln1_b[layer])
        u, g = jnp.split(x @ w_ffn_in[layer], 2, axis=-1)
        h = jax.nn.gelu(causal_dwconv(u, conv_w[layer], conv_b[layer])) * g
        x = layer_norm(DEEPNORM_ALPHA * x + h @ w_ffn_out[layer], ln2_g[layer], ln2_b[layer])
    return x
```

```python
import math
import contextlib
import numpy as np
import concourse.bass as bass
import concourse.mybir as mybir
from concourse.bass_utils import run_bass_kernel_spmd

F32 = mybir.dt.float32
BF16 = mybir.dt.bfloat16
AF = mybir.ActivationFunctionType
ALU = mybir.AluOpType
AX = mybir.AxisListType

D = 1024
NH = 8
HD = 64
DFF = 2816
NFC = DFF // 128
NIN = 5704
O_QA, O_KA, O_VA, O_QI, O_KI, O_WI, O_QS, O_KS, O_VS, O_GA, O_GB = (
    0, 512, 1024, 1536, 2048, 2112, 2120, 2632, 3144, 3656, 4680)
LN_EPS = 1e-5
ALPHA = 2.0 ** 0.25
NIT = 24
NEG = -1.0e30

ENGS = ["pe", "act", "dve", "pool", "sp"]


class Buf:
    __slots__ = ("name", "last_w", "readers", "psum")

    def __init__(self, name, psum=False):
        self.name = name
        self.last_w = None
        self.readers = []
        self.psum = psum


class Op:
    __slots__ = ("eng", "fn", "deps", "prio", "cidx", "slot", "dma_val", "ms", "is_target")


class Sched:
    def __init__(self, nc):
        self.nc = nc
        self.ops = []
        self.slots = {}
        self.prio_off = 0.0

    def op(self, eng, fn, reads=(), writes=(), slot=None, prio=None):
        o = Op()
        o.eng = eng
        o.fn = fn
        o.cidx = len(self.ops)
        o.prio = float(o.cidx) + self.prio_off if prio is None else prio
        o.slot = slot
        o.dma_val = None
        o.ms = None
        o.is_target = False
        deps = set()
        for b in reads:
            if b.last_w is not None:
                deps.add(b.last_w)
            if b.psum:
                for r in b.readers:
                    if r.eng != eng:
                        deps.add(r)
        for b in writes:
            if b.last_w is not None:
                deps.add(b.last_w)
            for r in b.readers:
                deps.add(r)
        for b in reads:
            b.readers.append(o)
        for b in writes:
            b.last_w = o
            b.readers = []
        deps.discard(o)
        o.deps = deps
        if slot is not None:
            s = self.slots.setdefault(slot, [None, 0])
            s[1] += 16
            o.dma_val = s[1]
        self.ops.append(o)
        return o

    def emit(self, final_wait_slots=()):
        nc = self.nc
        order = sorted(self.ops, key=lambda o: (o.prio, o.cidx))
        pos = {o: i for i, o in enumerate(order)}
        for o in order:
            for d in o.deps:
                assert pos[d] < pos[o], "priority order violates a dependency"
        queues = {e: [] for e in ENGS}
        for o in order:
            queues[o.eng].append(o)
        for o in order:
            for d in o.deps:
                if d.slot is None:
                    if d.eng == "pe" and o.eng == "pe":
                        continue
                    d.is_target = True
        for e in ENGS:
            n = 0
            for o in queues[e]:
                if o.slot is None and o.is_target:
                    n += 1
                    o.ms = n
        with contextlib.ExitStack() as es:
            esem = {e: es.enter_context(nc.semaphore("s_" + e)) for e in ENGS}
            for name, s in self.slots.items():
                s[0] = es.enter_context(nc.semaphore("d_" + name))
            block = es.enter_context(nc.Block())
            slots = self.slots

            def run(engname, eng):
                waited = {}
                for o in queues[engname]:
                    need = {}
                    for d in o.deps:
                        if d.slot is not None:
                            key = ("d", d.slot)
                            val = d.dma_val
                            sem = slots[d.slot][0]
                        else:
                            if d.eng == "pe" and engname == "pe":
                                continue
                            key = ("e", d.eng)
                            val = d.ms
                            sem = esem[d.eng]
                        if val > need.get(key, (None, 0))[1]:
                            need[key] = (sem, val)
                    for key, (sem, val) in need.items():
                        if waited.get(key, 0) >= val:
                            continue
                        eng.wait_ge(sem, val)
                        waited[key] = val
                    ins = o.fn(eng)
                    if o.slot is not None:
                        ins.then_inc(slots[o.slot][0], 16)
                    elif o.is_target:
                        ins.then_inc(esem[engname], 1)
                if engname == "sp":
                    for name in final_wait_slots:
                        s = slots[name]
                        eng.wait_ge(s[0], s[1])

            block.tensor(lambda eng: run("pe", eng))
            block.scalar(lambda eng: run("act", eng))
            block.vector(lambda eng: run("dve", eng))
            block.gpsimd(lambda eng: run("pool", eng))
            block.sync(lambda eng: run("sp", eng))
        self.stats = {e: len(queues[e]) for e in ENGS}


def _t5_bucket(rel):
    nb = 16
    max_exact = 8
    side = np.where(rel > 0, nb, 0)
    n = np.abs(rel)
    nf = np.maximum(n, 1).astype(np.float32)
    large = max_exact + (np.log(nf / max_exact) / math.log(128 / max_exact) * (nb - max_exact)).astype(np.int32)
    large = np.minimum(large, nb - 1)
    return side + np.where(n < max_exact, n, large)


def host_constants():
    s = np.arange(128)[:, None]
    t = np.arange(128)[None, :]
    c = {}
    c["c_ident"] = np.eye(128, dtype=np.float32)
    c["c_trineg"] = np.where(s >= t, -8.0, 0.0).astype(np.float32)
    c["c_causal"] = (s < t).astype(np.float32)
    c["c_dsaneg"] = np.where((np.arange(128)[None, :] // 64) > (np.arange(128)[:, None] // 64), NEG, 0.0).astype(np.float32)
    oh = []
    ohidx = []
    for kind in range(2):
        rel = (s - t) - 128 * kind
        bk = _t5_bucket(rel)
        for b in sorted(set(bk.flatten().tolist())):
            oh.append((bk == b).astype(np.float32))
            ohidx.append((kind, int(b)))
    c["c_oh"] = np.stack(oh, 0).transpose(1, 0, 2).copy()
    c["c_pow2"] = np.tile((2.0 ** -(np.arange(NIT + 1) + 1.0)).astype(np.float32)[None, :], (128, 1))
    return c, ohidx


_CONSTS, _OHIDX = host_constants()
N_OH = len(_OHIDX)


def build(nseq, S, debug=False, stop_after=99):
    assert S % 512 == 0
    NB = S // 128
    NSB = S // 512
    NSEL = min(256, S // 4)
    nc = bass.Bass("TRN2", target_bir_lowering=False)
    S_ = Sched(nc)
    op = S_.op

    def dram_in(name, shape, dt=F32):
        return nc.dram_tensor(name, list(shape), dt, kind="ExternalInput").ap()

    x = dram_in("x", [nseq, S, D])
    rel_bias = dram_in("rel_bias", [32, 8])
    w_in = dram_in("w_in", [D, NIN])
    w_bd = dram_in("w_branch_dsa", [512, D])
    w_bs = dram_in("w_branch_sb", [512, D])
    w_out = dram_in("w_out", [D, D])
    w_fi = dram_in("w_ffn_in", [D, 2 * DFF])
    w_fo = dram_in("w_ffn_out", [DFF, D])
    lnp = dram_in("lnp", [4, D])
    bg_d = dram_in("bg_t", [128, 16])
    cw_d = dram_in("cw_t", [128, NFC, 3])
    cb_d = dram_in("cb_t", [128, NFC])
    cd = {k: dram_in(k, v.shape) for k, v in _CONSTS.items()}
    out = nc.dram_tensor("out", [nseq, S, D], F32, kind="ExternalOutput").ap()
    dbg = {}

    def dbg_out(name, shape):
        dbg[name] = nc.dram_tensor(name, list(shape), F32, kind="ExternalOutput").ap()
        return dbg[name]

    SB_BASE = 16512 + 2048
    cursor = [SB_BASE]

    def alloc(name, shape, dt, at=None):
        nbytes = int(np.prod(shape[1:])) * (2 if dt == BF16 else 4)
        nbytes = (nbytes + 63) // 64 * 64
        if at is None:
            off = cursor[0]
            cursor[0] += nbytes
        else:
            off = at[0]
            at[0] += nbytes
        return nc.alloc_sbuf_tensor_at(name, list(shape), dt, offset=off, align_bytes=64).ap()

    ident = alloc("ident", [128, 128], BF16)
    identf = alloc("identf", [128, 128], F32)
    trineg = alloc("trineg", [128, 128], BF16)
    onesneg = alloc("onesneg", [128, 128], BF16)
    causal = alloc("causal", [128, 128], BF16)
    dsaneg = alloc("dsaneg", [128, 128], F32)
    zeros = alloc("zeros", [128, 512], BF16)
    bias8T = alloc("bias8T", [128, 16, 128], BF16)
    rbb = alloc("rbb", [128, 256], F32)
    pow2 = alloc("pow2", [128, NIT + 1], F32)
    bg = alloc("bg", [128, 16], F32)
    cw = alloc("cw", [128, NFC, 3], F32)
    cbv = alloc("cbv", [128, NFC], F32)
    small = alloc("small", [128, 64], F32)
    y_aT = alloc("y_aT", [128, 4, S], BF16)
    qiT = alloc("qiT", [128, 4, S], BF16)
    y_sT = qiT
    wbuf = [alloc("wbuf%d" % i, [128, 8, 512], BF16) for i in range(2)]
    xtok = [alloc("xtok%d" % i, [128, 1024], BF16) for i in range(2)]
    region0 = cursor[0]
    ra = [region0]
    xT_off = [ra[0]]
    xT = alloc("xT", [128, 8, S], BF16, ra)
    oh = alloc("oh", [128, N_OH, 128], BF16, xT_off)
    bacc = alloc("bacc", [128, 128], F32, xT_off)
    qT = alloc("qT", [128, 4, S], BF16, ra)
    kT = alloc("kT", [128, 4, S], BF16, ra)
    kiT = alloc("kiT", [128, S], BF16, ra)
    vaug = alloc("vaug", [128, NB, NH, 66], BF16, ra)
    wi = alloc("wi", [128, NB, 8], F32, ra)
    wki = alloc("wki", [128, 8, 128], BF16, ra)
    wwi = alloc("wwi", [128, 8, 8], BF16, ra)
    sc_off = [ra[0]]
    sc = alloc("sc", [128, S], F32, ra)
    rl = [alloc("rl%d" % i, [128, 512], F32, ra) for i in range(2)]
    mask = alloc("mask", [128, S], BF16, ra)
    junk = mask
    maskT = alloc("maskT", [128, NB, 512], BF16, ra)
    ex = [alloc("ex%d" % i, [128, 512], BF16, ra) for i in range(2)]
    e1 = [alloc("e1_%d" % i, [128, 512], F32, sc_off) for i in range(2)]
    sp_ = [alloc("sp%d" % i, [128, 512], BF16, sc_off) for i in range(2)]
    PT = [alloc("PT%d" % i, [128, 512], BF16, ra) for i in range(2)]
    Rb = alloc("Rb", [128, 512], BF16, sc_off)
    assert sc_off[0] <= ra[0]
    ytok = alloc("ytok", [128, 4, 512], BF16, ra)
    bis = alloc("bis", [128, 8 + NIT + 1], F32, ra)
    rec = alloc("rec", [128, 4], F32, ra)
    rf = [region0]
    lnt = alloc("lnt", [128, 4, D], F32, rf)
    xr = alloc("xr", [128, 4, D], F32, rf)
    xTt = alloc("xTt", [128, 8, 512], BF16, rf)
    gtmp = [alloc("gtmp%d" % i, [128, 512], BF16, rf) for i in range(2)]
    mtmp = alloc("mtmp", [128, 512], F32, rf)
    mT = alloc("mT", [128, 8, 512], BF16, rf)
    x1T = xTt
    wbr = alloc("wbr", [128, 4, D], BF16, rf)
    hT = alloc("hT", [128, NFC, 512], BF16, rf)
    wfo = alloc("wfo", [128, NFC, D], BF16, rf)
    halo = alloc("halo", [128, NFC, 2], F32, rf)
    cbuf = [alloc("cbuf%d" % i, [128, 512], F32, rf) for i in range(2)]
    t2 = [alloc("t2_%d" % i, [128, 512], F32, rf) for i in range(2)]
    lnsm = alloc("lnsm", [128, 32], F32, rf)
    sb_limit = nc.SBUF_PARTITION_SIZE_BYTES
    print("SBUF bytes: common", region0, "att", ra[0], "ffn", rf[0])
    assert max(ra[0], rf[0]) <= nc.SBUF_PARTITION_SIZE_BYTES, (ra[0], rf[0])

    banks = [nc.alloc_psum_tensor("bank%d" % i, [128, 512], F32).ap() for i in range(6)]
    banksbf = {i: nc.alloc_psum_tensor("bankbf%d" % i, [128, 8, 128], BF16).ap() for i in (6, 7)}
    BK = [Buf("bank%d" % i, psum=True) for i in range(8)]

    B = {}
    last_barrier = [None]

    def bf(name):
        if name not in B:
            B[name] = Buf(name)
            B[name].last_w = last_barrier[0]
        return B[name]

    def mm(o, l, r, start, stop, reads, writes, **kw):
        return op("pe", lambda e: e.matmul(o, lhsT=l, rhs=r, start=start, stop=stop, **kw), reads, writes)

    def tr(o, i, idn, reads, writes):
        return op("pe", lambda e: e.transpose(o, i, idn), reads, writes)

    def act(o, i, func, reads, writes, **kw):
        return op("act", lambda e: e.activation(o, i, func, **kw), reads, writes)

    def dma(eng, o, i, reads, writes, slot):
        return op(eng, lambda e: e.dma_start(out=o, in_=i), reads, writes, slot=slot)

    def ts(eng, o, i, s1, s2, op0, op1=None, reads=(), writes=(), accum=None):
        if op1 is None:
            return op(eng, lambda e: e.tensor_scalar(o, i, s1, None, op0), reads, writes)
        if accum is None:
            return op(eng, lambda e: e.tensor_scalar(o, i, s1, s2, op0, op1), reads, writes)
        return op(eng, lambda e: e.tensor_scalar(o, i, s1, s2, op0, op1, accum_out=accum), reads, writes)

    def stt(o, i0, sc_, i1, op0, op1, reads, writes):
        return op("dve", lambda e: e.scalar_tensor_tensor(o, i0, sc_, i1, op0, op1), reads, writes)

    def tt(eng, o, i0, i1, aop, reads, writes):
        return op(eng, lambda e: e.tensor_tensor(o, i0, i1, aop), reads, writes)

    def cp(eng, o, i, reads, writes):
        if eng == "act":
            return op("act", lambda e: e.copy(o, i), reads, writes)
        return op(eng, lambda e: e.tensor_copy(o, i), reads, writes)

    evq = [0]

    def evac(o, i, reads, writes):
        evq[0] ^= 1
        return cp("act" if evq[0] else "dve", o, i, reads, writes)

    def setup():
        dma("pool", ident, cd["c_ident"], [], [bf("ident")], "c0")
        dma("sp", identf, cd["c_ident"], [], [bf("identf")], "c1")
        dma("pool", trineg, cd["c_trineg"], [], [bf("trineg")], "c2")
        dma("pool", causal, cd["c_causal"], [], [bf("causal")], "c3")
        dma("sp", dsaneg, cd["c_dsaneg"], [], [bf("dsaneg")], "c4")
        dma("sp", pow2, cd["c_pow2"], [], [bf("pow2")], "c5")
        dma("sp", bg, bg_d, [], [bf("bg")], "c6")
        dma("sp", cw, cw_d, [], [bf("cw")], "c7")
        dma("sp", cbv, cb_d, [], [bf("cbv")], "c8")
        dma("sp", rbb, rel_bias.rearrange("a b -> (a b)").partition_broadcast(128), [], [bf("rbb")], "c9")
        dma("pool", oh, cd["c_oh"], [], [bf("oh")], "c10")
        op("dve", lambda e: e.memset(onesneg, -8.0), [], [bf("onesneg")])
        op("dve", lambda e: e.memset(zeros, 0.0), [], [bf("zeros")])
        op("pool", lambda e: e.memset(vaug, 1.0), [], [bf("vaug_all")])
        for h in range(NH):
            for kind in range(2):
                idxs = [i for i, (k, b) in enumerate(_OHIDX) if k == kind]
                first = True
                for i in idxs:
                    b = _OHIDX[i][1]
                    scal = rbb[:, b * 8 + h:b * 8 + h + 1]
                    if first:
                        ts("dve", bacc, oh[:, i, :], scal, None, ALU.mult, reads=[bf("oh"), bf("rbb")], writes=[bf("bacc")])
                        first = False
                    else:
                        stt(bacc, oh[:, i, :], scal, bacc, ALU.mult, ALU.add, [bf("oh"), bf("rbb"), bf("bacc")], [bf("bacc")])
                ts("dve", bias8T[:, h * 2 + kind, :], bacc, rbb[:, 120 + h:121 + h], 8.0, ALU.subtract, ALU.mult,
                   reads=[bf("bacc"), bf("rbb")], writes=[bf("bias8T")])
        op("dve", lambda e: e.memset(small[:, 0:1], 0.0), [bf("oh"), bf("bacc"), bf("bias8T")], [])

    wslot = [0]

    def load_w(src_ap, ncols, kchunks=8):
        i = wslot[0]
        wslot[0] ^= 1
        dst = wbuf[i].rearrange("p c n -> p (c n)")[:, 0:kchunks * ncols].rearrange("p (c n) -> p c n", c=kchunks)
        dma("pool", dst, src_ap.rearrange("(c p) n -> p c n", p=128), [], [bf("wbuf%d" % i)], "w%d" % i)
        return dst, bf("wbuf%d" % i)

    bankrr = [0]

    def next_bank(lo=0, hi=4):
        b = lo + bankrr[0] % (hi - lo)
        bankrr[0] += 1
        return b

    def phase_x(b):
        for tb in range(NB):
            i = tb % 2
            dma("pool", xtok[i], x[b, tb * 128:(tb + 1) * 128, :], [], [bf("xtok%d" % i)], "xtok%d" % i)
            bk = 6 + i
            for cc in range(8):
                tr(banksbf[bk][:, cc, :], xtok[i][:, cc * 128:(cc + 1) * 128], ident,
                   [bf("xtok%d" % i), bf("ident")], [BK[bk]])
            evac(xT[:, :, tb * 128:(tb + 1) * 128], banksbf[bk][:, :, :], [BK[bk]], [bf("xT%d" % tb)])

    def proj_feat(wt, wb, ncol_chunks, dst, dstname, col0=0):
        for j in range(ncol_chunks):
            for tt_ in range(NSB):
                bk = next_bank()
                for cc in range(8):
                    mm(banks[bk][:, :], wt[:, cc, col0 + j * 128:col0 + (j + 1) * 128], xT[:, cc, tt_ * 512:(tt_ + 1) * 512],
                       cc == 0, cc == 7, [wb] + [bf("xT%d" % (tt_ * 4 + q)) for q in range(4)], [BK[bk]])
                evac(dst[:, j, tt_ * 512:(tt_ + 1) * 512], banks[bk][:, :], [BK[bk]], [bf("%s_%d_%d" % (dstname, j, tt_))])

    def proj_v(wt, wb):
        for tb in range(NB):
            bk = next_bank()
            for cc in range(8):
                mm(banks[bk][:, :], xT[:, cc, tb * 128:(tb + 1) * 128], wt[:, cc, 0:512], cc == 0, cc == 7,
                   [wb, bf("xT%d" % tb)], [BK[bk]])
            evac(vaug[:, tb, :, 0:64], banks[bk][:, :].rearrange("p (h d) -> p h d", h=NH), [BK[bk], bf("vaug_all")], [bf("vaug%d" % tb)])

    def phase_proj_dsa(b):
        wt, wb = load_w(w_in[:, O_QA:O_QA + 512], 512)
        proj_feat(wt, wb, 4, qT, "qT")
        wt, wb = load_w(w_in[:, O_KA:O_KA + 512], 512)
        proj_feat(wt, wb, 4, kT, "kT")
        wt, wb = load_w(w_in[:, O_VA:O_VA + 512], 512)
        proj_v(wt, wb)
        wt, wb = load_w(w_in[:, O_QI:O_QI + 512], 512)
        proj_feat(wt, wb, 4, qiT, "qiT")
        src = w_in[:, O_KI:O_KI + 64].rearrange("(c p) n -> p c n", p=128)
        dma("pool", wki[:, :, 0:64], src, [], [bf("wki")], "wki")
        dma("pool", wki[:, :, 64:128], src, [], [bf("wki")], "wki")
        dma("pool", wwi, w_in[:, O_WI:O_WI + 8].rearrange("(c p) n -> p c n", p=128), [], [bf("wwi")], "wwi")
        for tt_ in range(NSB):
            bk = next_bank()
            for cc in range(8):
                mm(banks[bk][:, :], wki[:, cc, :], xT[:, cc, tt_ * 512:(tt_ + 1) * 512], cc == 0, cc == 7,
                   [bf("wki")] + [bf("xT%d" % (tt_ * 4 + q)) for q in range(4)], [BK[bk]])
            evac(kiT[:, tt_ * 512:(tt_ + 1) * 512], banks[bk][:, :], [BK[bk]], [bf("kiT_%d" % tt_)])
        for tb in range(NB):
            bk = next_bank()
            for cc in range(8):
                mm(banks[bk][:, 0:8], xT[:, cc, tb * 128:(tb + 1) * 128], wwi[:, cc, :], cc == 0, cc == 7,
                   [bf("wwi"), bf("xT%d" % tb)], [BK[bk]])
            op("act", lambda e, o_=wi[:, tb, :], i_=banks[bk][:, 0:8]: e.mul(o_, i_, float(8 ** -0.5)), [BK[bk]], [bf("wi%d" % tb)])

    def phase_proj_sb(b):
        wt, wb = load_w(w_in[:, O_QS:O_QS + 512], 512)
        proj_feat(wt, wb, 4, qT, "qT")
        wt, wb = load_w(w_in[:, O_KS:O_KS + 512], 512)
        proj_feat(wt, wb, 4, kT, "kT")
        wt, wb = load_w(w_in[:, O_VS:O_VS + 512], 512)
        proj_v(wt, wb)

    def hrows(h):
        r0 = (h % 2) * 64
        return slice(r0, r0 + 64), h // 2

    def dsa_scores(b, tb):
        nk = (tb + 1) * 128
        qs = tb // 4
        scb = bf("sc")
        nkt = (nk + 511) // 512
        for kt in range(nkt):
            k0 = kt * 512
            n = min(512, nk - k0)
            for h in range(NH):
                rs, ch = hrows(h)
                bk = next_bank()
                mm(banks[bk][:, 0:n], qiT[rs, ch, tb * 128:(tb + 1) * 128], kiT[rs, k0:k0 + n], True, True,
                   [bf("qiT_%d_%d" % (ch, qs)), bf("kiT_%d" % kt)], [BK[bk]])
                r = h % 2
                act(rl[r][:, 0:n], banks[bk][:, 0:n], AF.Relu, [BK[bk]], [bf("rl%d" % r)], scale=0.125)
                wsc = wi[:, tb, h:h + 1]
                if h == 0:
                    ts("dve", sc[:, k0:k0 + n], rl[r][:, 0:n], wsc, None, ALU.mult, reads=[bf("rl%d" % r), bf("wi%d" % tb)], writes=[scb])
                else:
                    stt(sc[:, k0:k0 + n], rl[r][:, 0:n], wsc, sc[:, k0:k0 + n], ALU.mult, ALU.add,
                        [bf("rl%d" % r), bf("wi%d" % tb), scb], [scb])
        bb = bf("bis")
        op("dve", lambda e: e.tensor_reduce(bis[:, 0:1], sc[:, 0:nk], AX.X, ALU.max), [scb], [bb])
        op("dve", lambda e: e.tensor_reduce(bis[:, 1:2], sc[:, 0:nk], AX.X, ALU.min), [scb, bb], [bb])
        tt("dve", sc[:, tb * 128:nk], sc[:, tb * 128:nk], dsaneg, ALU.add, [scb, bf("dsaneg")], [scb])
        ts("dve", bis[:, 2:3], bis[:, 0:1], bis[:, 1:2], 1.002, ALU.subtract, ALU.mult, reads=[bb], writes=[bb])
        tt("dve", bis[:, 1:2], bis[:, 0:1], bis[:, 2:3], ALU.subtract, [bb], [bb])
        ts("dve", bis[:, 8:8 + NIT + 1], pow2, bis[:, 2:3], None, ALU.mult, reads=[bb, bf("pow2")], writes=[bb])
        tt("dve", bis[:, 3:4], bis[:, 1:2], bis[:, 8:9], ALU.add, [bb], [bb])
        jb = bf("mask")
        for it in range(NIT):
            ts("dve", junk[:, 0:nk], sc[:, 0:nk], bis[:, 3:4], 0.0, ALU.is_ge, ALU.add, reads=[scb, bb], writes=[jb, bb],
               accum=bis[:, 4:5])
            stt(bis[:, 5:6], bis[:, 4:5], float(NSEL) - 0.5, bis[:, 8 + it:9 + it], ALU.is_ge, ALU.mult, [bb], [bb])
            stt(bis[:, 3:4], bis[:, 5:6], bis[:, 9 + it:10 + it], bis[:, 3:4], ALU.subtract, ALU.add, [bb], [bb])
        tt("dve", bis[:, 6:7], bis[:, 3:4], bis[:, 8 + NIT:9 + NIT], ALU.subtract, [bb], [bb])
        ts("dve", mask[:, 0:nk], sc[:, 0:nk], bis[:, 6:7], None, ALU.is_ge, reads=[scb, bb], writes=[bf("mask")])
        j = tb % 4
        nsb = tb + 1
        for g0 in range(0, nsb, 8):
            g1 = min(nsb, g0 + 8)
            bk = 6 + (g0 // 8 + tb) % 2
            for sb in range(g0, g1):
                tr(banksbf[bk][:, sb - g0, :], mask[:, sb * 128:(sb + 1) * 128], ident, [bf("mask"), bf("ident")], [BK[bk]])
            evac(maskT[:, g0:g1, j * 128:(j + 1) * 128], banksbf[bk][:, 0:g1 - g0, :], [BK[bk]], [bf("maskT%d" % j)])

    def yt_transposes(b, qs, dstT, dstname):
        for j in range(4):
            bk = 6 + j % 2
            for c in range(4):
                tr(banksbf[bk][:, c, :], ytok[:, j, c * 128:(c + 1) * 128], ident, [bf("ytok%d" % j), bf("ident")], [BK[bk]])
            tb = qs * 4 + j
            evac(dstT[:, :, tb * 128:(tb + 1) * 128], banksbf[bk][:, 0:4, :], [BK[bk]], [bf("%s%d" % (dstname, tb))])

    def dsa_attend(b, qs):
        nsb_tot = qs * 4 + 4
        for h in range(NH):
            rs, ch = hrows(h)
            ybk = 4 + h % 2
            mm(banks[ybk][:, 0:260], zeros[:, 0:128], zeros[:, 0:260], True, False, [bf("zeros")], [BK[ybk]])
            for sb in range(nsb_tot):
                j0 = max(0, sb - qs * 4)
                c0 = j0 * 128
                n = 512 - c0
                lbk = next_bank()
                near = [(jj, sb == qs * 4 + jj) for jj in range(j0, 4) if (qs * 4 + jj) - sb in (0, 1)]
                mm(banks[lbk][:, c0:512], kT[rs, ch, sb * 128:(sb + 1) * 128], qT[rs, ch, qs * 512 + c0:qs * 512 + 512],
                   True, len(near) == 0, [bf("kT_%d_%d" % (ch, sb // 4)), bf("qT_%d_%d" % (ch, qs))], [BK[lbk]])
                for idx, (jj, isdiag) in enumerate(near):
                    kind = 0 if isdiag else 1
                    mm(banks[lbk][:, jj * 128:(jj + 1) * 128], ident, bias8T[:, h * 2 + kind, :], False, idx == len(near) - 1,
                       [bf("ident"), bf("bias8T")], [BK[lbk]])
                r = sb % 2
                act(ex[r][:, c0:512], banks[lbk][:, c0:512], AF.Exp, [BK[lbk], bf("rbb")], [bf("ex%d" % r)],
                    scale=0.125, bias=rbb[:, 120 + h:121 + h])
                eng = "dve" if (sb % 2 == 0) else "pool"
                tt(eng, PT[r][:, c0:512], ex[r][:, c0:512], maskT[:, sb, c0:512], ALU.mult,
                   [bf("ex%d" % r)] + [bf("maskT%d" % jj) for jj in range(j0, 4)], [bf("PT%d" % r)])
                for jj in range(j0, 4):
                    if stop_after < 3.5:
                        break
                    last = (sb == nsb_tot - 1) and (jj == 3)
                    mm(banks[ybk][:, jj * 65:(jj + 1) * 65], PT[r][:, jj * 128:(jj + 1) * 128], vaug[:, sb, h, 0:65], False, last,
                       [bf("PT%d" % r), bf("vaug%d" % sb)], [BK[ybk]])
            if stop_after < 3.8:
                continue
            yv = banks[ybk][:, 0:260].rearrange("p (j e) -> p j e", e=65)
            rb_ = bf("rec")
            op("dve", lambda e, yv=yv: e.reciprocal(rec[:, 0:4], yv[:, :, 64]), [BK[ybk]], [rb_])
            for jj in range(4):
                ts("dve", ytok[:, jj, h * 64:(h + 1) * 64], yv[:, jj, 0:64], rec[:, jj:jj + 1], None, ALU.mult,
                   reads=[BK[ybk], rb_], writes=[bf("ytok%d" % jj)])
        if stop_after < 3.8:
            return
        yt_transposes(b, qs, y_aT, "y_aT")

    TLE = "dve"
    SBE = "dve"

    def sb_attend(b, qs):
        nsb_tot = qs * 4 + 4
        Rbuf = bf("Rb")
        for h in range(NH):
            rs, ch = hrows(h)
            ybk = 4 + h % 2
            mm(banks[ybk][:, 0:260], zeros[:, 0:128], zeros[:, 0:260], True, False, [bf("zeros")], [BK[ybk]])
            op(SBE, lambda e: e.memset(Rb, 0.0), [], [Rbuf])
            for sb in range(nsb_tot - 1, -1, -1):
                j0 = max(0, sb - qs * 4)
                c0 = j0 * 128
                diag = sb >= qs * 4
                zbk = next_bank()
                r = sb % 2
                first = (sb == nsb_tot - 1)
                mm(banks[zbk][:, c0:512], kT[rs, ch, sb * 128:(sb + 1) * 128], qT[rs, ch, qs * 512 + c0:qs * 512 + 512],
                   True, True, [bf("kT_%d_%d" % (ch, sb // 4)), bf("qT_%d_%d" % (ch, qs))], [BK[zbk]])
                act(e1[r][:, c0:512], banks[zbk][:, c0:512], AF.Exp, [BK[zbk]], [bf("e1_%d" % r)], scale=0.125)
                ts("dve", e1[r][:, c0:512], e1[r][:, c0:512], 1.0, None, ALU.add, reads=[bf("e1_%d" % r)], writes=[bf("e1_%d" % r)])
                act(sp_[r][:, c0:512], e1[r][:, c0:512], AF.Ln, [bf("e1_%d" % r)], [bf("sp%d" % r)])
                if diag:
                    tt(SBE, sp_[r][:, c0:c0 + 128], sp_[r][:, c0:c0 + 128], causal, ALU.mult,
                       [bf("sp%d" % r), bf("causal")], [bf("sp%d" % r)])
                mm(banks[zbk][:, c0:512], trineg, sp_[r][:, c0:512], False, True, [bf("trineg"), bf("sp%d" % r)], [BK[zbk]],
                   skip_group_check=True)
                if not first:
                    mm(banks[zbk][:, c0:512], onesneg, Rb[:, c0:512], False, True, [bf("onesneg"), Rbuf], [BK[zbk]],
                       skip_group_check=True)
                act(PT[r][:, c0:512], banks[zbk][:, c0:512], AF.Exp, [BK[zbk]], [bf("PT%d" % r)], scale=0.125)
                if diag:
                    tt("dve", PT[r][:, c0:c0 + 128], PT[r][:, c0:c0 + 128], causal, ALU.mult,
                       [bf("PT%d" % r), bf("causal")], [bf("PT%d" % r)])
                if sb > 0:
                    tt(SBE, Rb[:, c0:512], Rb[:, c0:512], sp_[r][:, c0:512], ALU.add, [Rbuf, bf("sp%d" % r)], [Rbuf])
                for jj in range(j0, 4):
                    last = (sb == 0) and (jj == 3)
                    mm(banks[ybk][:, jj * 65:(jj + 1) * 65], PT[r][:, jj * 128:(jj + 1) * 128], vaug[:, sb, h, 0:65], False, last,
                       [bf("PT%d" % r), bf("vaug%d" % sb)], [BK[ybk]])
            yv = banks[ybk][:, 0:260].rearrange("p (j e) -> p j e", e=65)
            for jj in range(4):
                cp("act", ytok[:, jj, h * 64:(h + 1) * 64], yv[:, jj, 0:64], [BK[ybk]], [bf("ytok%d" % jj)])
        yt_transposes(b, qs, y_sT, "y_sT")

    def layer_norm(eng_hint, j, gi, dst_bufs, src_reads):
        lb = bf("lnsm")
        xb = bf("xr%d" % j)
        for hh in range(2):
            op("dve", lambda e, hh=hh: e.bn_stats(lnsm[:, hh * 6:(hh + 1) * 6], xr[:, j, hh * 512:(hh + 1) * 512]), [xb], [lb])
        op("dve", lambda e: e.bn_aggr(lnsm[:, 12:14], lnsm[:, 0:12]), [lb], [lb])
        ts("dve", lnsm[:, 14:15], lnsm[:, 13:14], LN_EPS, None, ALU.add, reads=[lb], writes=[lb])
        op("act", lambda e: e.sqrt(lnsm[:, 15:16], lnsm[:, 14:15]), [lb], [lb])
        op("dve", lambda e: e.reciprocal(lnsm[:, 16:17], lnsm[:, 15:16]), [lb], [lb])
        ts("dve", xr[:, j, :], xr[:, j, :], lnsm[:, 12:13], lnsm[:, 16:17], ALU.subtract, ALU.mult, reads=[xb, lb], writes=[xb])
        tt(TLE, xr[:, j, :], xr[:, j, :], lnt[:, gi, :], ALU.mult, [xb, bf("lnt")], [xb])
        tt(TLE, xr[:, j, :], xr[:, j, :], lnt[:, gi + 1, :], ALU.add, [xb, bf("lnt")], [xb])

    def phase_tail(b, tt_):
        t0 = tt_ * 512
        for j in range(4):
            dma("sp", xr[:, j, :], x[b, t0 + j * 128:t0 + (j + 1) * 128, :], [], [bf("xr%d" % j)], "xr%d" % j)
        for j in range(4):
            for g in range(2):
                bk = next_bank()
                for c in range(4):
                    cc = g * 4 + c
                    tr(banks[bk][:, c * 128:(c + 1) * 128], xr[:, j, cc * 128:(cc + 1) * 128], identf, [bf("xr%d" % j), bf("identf")], [BK[bk]])
                evac(xTt[:, g * 4:(g + 1) * 4, j * 128:(j + 1) * 128], banks[bk][:, :].rearrange("p (c t) -> p c t", c=4), [BK[bk]], [bf("xTt")])
        for br in range(2):
            wsrc = w_bd if br == 0 else w_bs
            yT_ = y_aT if br == 0 else y_sT
            yname = "y_aT" if br == 0 else "y_sT"
            goff = O_GA if br == 0 else O_GB
            wbt, wbb = wbr, bf("wbr")
            dma("pool", wbr, wsrc.rearrange("(c p) n -> p c n", p=128), [], [wbb], "wbr")
            for half in range(2):
                wgt, wgb = load_w(w_in[:, goff + half * 512:goff + (half + 1) * 512], 512)
                for jc in range(4):
                    jn = half * 4 + jc
                    gbk = next_bank()
                    for cc in range(8):
                        mm(banks[gbk][:, :], wgt[:, cc, jc * 128:(jc + 1) * 128], xTt[:, cc, :], cc == 0, cc == 7, [wgb, bf("xTt")], [BK[gbk]])
                    r = jn % 2
                    act(gtmp[r], banks[gbk][:, :], AF.Sigmoid, [BK[gbk], bf("bg")], [bf("gtmp%d" % r)], bias=bg[:, br * 8 + jn:br * 8 + jn + 1])
                    bbk = next_bank()
                    for k in range(4):
                        mm(banks[bbk][:, :], wbt[:, k, jn * 128:(jn + 1) * 128],
                           yT_[:, k, t0:t0 + 512], k == 0, k == 3, [wbb] + [bf("%s%d" % (yname, tt_ * 4 + q)) for q in range(4)], [BK[bbk]])
                    if br == 0:
                        tt("dve", mT[:, jn, :], gtmp[r], banks[bbk][:, :], ALU.mult, [bf("gtmp%d" % r), BK[bbk]], [bf("mT%d" % jn)])
                    else:
                        tt("dve", mtmp, gtmp[r], banks[bbk][:, :], ALU.mult, [bf("gtmp%d" % r), BK[bbk]], [bf("mtmp")])
                        tt(TLE, mT[:, jn, :], mT[:, jn, :], mtmp, ALU.add, [bf("mtmp"), bf("mT%d" % jn)], [bf("mT%d" % jn)])
        for nh in range(2):
            wot, wob = load_w(w_out[:, nh * 512:(nh + 1) * 512], 512)
            for j in range(4):
                bk = next_bank()
                for k in range(8):
                    mm(banks[bk][:, :], mT[:, k, j * 128:(j + 1) * 128], wot[:, k, 0:512], k == 0, k == 7, [wob, bf("mT%d" % k)], [BK[bk]])
                stt(xr[:, j, nh * 512:(nh + 1) * 512], xr[:, j, nh * 512:(nh + 1) * 512], ALPHA, banks[bk][:, :], ALU.mult, ALU.add,
                    [bf("xr%d" % j), BK[bk]], [bf("xr%d" % j)])
        for j in range(4):
            layer_norm("dve", j, 0, None, None)
        if debug and tt_ == 0 and b == 0:
            dma("sp", dbg["d_x1"], xr[:, 0, :], [bf("xr0")], [], "dbg")
        for j in range(4):
            for g in range(2):
                bk = next_bank()
                for c in range(4):
                    cc = g * 4 + c
                    tr(banks[bk][:, c * 128:(c + 1) * 128], xr[:, j, cc * 128:(cc + 1) * 128], identf, [bf("xr%d" % j), bf("identf")], [BK[bk]])
                evac(x1T[:, g * 4:(g + 1) * 4, j * 128:(j + 1) * 128], banks[bk][:, :].rearrange("p (c t) -> p c t", c=4), [BK[bk]], [bf("xTt")])
        for q in range(0, NFC, 4):
            q1 = min(NFC, q + 4)
            dma("pool", wfo[:, q:q1, :], w_fo[q * 128:q1 * 128, :].rearrange("(c p) n -> p c n", p=128), [], [bf("wfo%d" % (q // 4))], "wfo%d" % (q // 4))
        for g0 in range(0, NFC, 4):
            g1 = min(NFC, g0 + 4)
            ncol = (g1 - g0) * 128
            wut, wub = load_w(w_fi[:, g0 * 128:g0 * 128 + ncol], ncol)
            wgt, wgb = load_w(w_fi[:, DFF + g0 * 128:DFF + g0 * 128 + ncol], ncol)
            for fc in range(g0, g1):
                lc = fc - g0
                ubk = next_bank()
                for cc in range(8):
                    mm(banks[ubk][:, :], wut[:, cc, lc * 128:(lc + 1) * 128], x1T[:, cc, :], cc == 0, cc == 7, [wub, bf("xTt")], [BK[ubk]])
                gbk = next_bank()
                for cc in range(8):
                    mm(banks[gbk][:, :], wgt[:, cc, lc * 128:(lc + 1) * 128], x1T[:, cc, :], cc == 0, cc == 7, [wgb, bf("xTt")], [BK[gbk]])
                r = fc % 2
                cbf = bf("cbuf%d" % r)
                hb = bf("halo")
                U = banks[ubk]
                act(cbuf[r], U[:, :], AF.Identity, [BK[ubk], bf("cw"), bf("cbv")], [cbf], scale=cw[:, fc, 2:3], bias=cbv[:, fc:fc + 1])
                stt(cbuf[r][:, 1:512], U[:, 0:511], cw[:, fc, 1:2], cbuf[r][:, 1:512], ALU.mult, ALU.add, [BK[ubk], cbf, bf("cw")], [cbf])
                stt(cbuf[r][:, 2:512], U[:, 0:510], cw[:, fc, 0:1], cbuf[r][:, 2:512], ALU.mult, ALU.add, [BK[ubk], cbf, bf("cw")], [cbf])
                if tt_ > 0:
                    stt(cbuf[r][:, 0:1], halo[:, fc, 1:2], cw[:, fc, 1:2], cbuf[r][:, 0:1], ALU.mult, ALU.add, [hb, cbf, bf("cw")], [cbf])
                    stt(cbuf[r][:, 0:2], halo[:, fc, 0:2], cw[:, fc, 0:1], cbuf[r][:, 0:2], ALU.mult, ALU.add, [hb, cbf, bf("cw")], [cbf])
                cp("dve", halo[:, fc, :], U[:, 510:512], [BK[ubk], cbf], [hb])
                op("act", lambda e, r=r: e.activation(t2[r], cbuf[r], AF.Square), [cbf], [bf("t2_%d" % r)])
                ts("dve", t2[r], t2[r], 0.0713548163, 1.5957691216, ALU.mult, ALU.add, reads=[bf("t2_%d" % r)], writes=[bf("t2_%d" % r)])
                tt(TLE, t2[r], t2[r], cbuf[r], ALU.mult, [bf("t2_%d" % r), cbf], [bf("t2_%d" % r)])
                act(t2[r], t2[r], AF.Sigmoid, [bf("t2_%d" % r)], [bf("t2_%d" % r)])
                tt(TLE, t2[r], t2[r], cbuf[r], ALU.mult, [bf("t2_%d" % r), cbf], [bf("t2_%d" % r)])
                tt("dve", hT[:, fc, :], t2[r], banks[gbk][:, :], ALU.mult, [bf("t2_%d" % r), BK[gbk]], [bf("hT%d" % fc)])
        for j in range(4):
            for nh in range(2):
                bk = next_bank()
                for fc in range(NFC):
                    mm(banks[bk][:, :], hT[:, fc, j * 128:(j + 1) * 128], wfo[:, fc, nh * 512:(nh + 1) * 512], fc == 0, fc == NFC - 1,
                       [bf("hT%d" % fc), bf("wfo%d" % (fc // 4))], [BK[bk]])
                stt(xr[:, j, nh * 512:(nh + 1) * 512], xr[:, j, nh * 512:(nh + 1) * 512], ALPHA, banks[bk][:, :], ALU.mult, ALU.add,
                    [bf("xr%d" % j), BK[bk]], [bf("xr%d" % j)])
            layer_norm("dve", j, 2, None, None)
            dma("sp", out[b, t0 + j * 128:t0 + (j + 1) * 128, :], xr[:, j, :], [bf("xr%d" % j)], [], "out%d" % j)

    def barrier():
        o = op("dve", lambda e: e.memset(small[:, 1:2], 0.0), [], list(B.values()) + BK)
        last_barrier[0] = o

    if debug:
        dbg_out("d_x1", [128, D])
        dbg_out("d_yaT", [128, 4, S])
        dbg_out("d_ysT", [128, 4, S])

    setup()
    for b in range(nseq):
        if stop_after < 1:
            break
        barrier()
        op("pool", lambda e: e.memset(vaug[:, :, :, 64:66], 1.0), [], [bf("vaug_all")])
        phase_x(b)
        if stop_after < 2:
            break
        phase_proj_dsa(b)
        if stop_after < 3:
            break
        for qs in range(NSB):
            for tb in range(qs * 4, qs * 4 + 4):
                dsa_scores(b, tb)
            if stop_after > 3:
                dsa_attend(b, qs)
        if stop_after < 4.4:
            break
        barrier()
        phase_proj_sb(b)
        for qs in range(NSB):
            if stop_after > 4.7:
                sb_attend(b, qs)
        if debug and b == 0:
            dma("pool", dbg["d_yaT"], y_aT, [bf("y_aT%d" % q) for q in range(NB)], [], "dbgp")
            dma("pool", dbg["d_ysT"], y_sT, [bf("y_sT%d" % q) for q in range(NB)], [], "dbgp")
        if stop_after < 6:
            break
        barrier()
        dma("sp", lnt, lnp.partition_broadcast(128), [], [bf("lnt")], "lnt")
        for tt_ in range(NSB):
            phase_tail(b, tt_)
    if stop_after < 6:
        barrier()
        dma("sp", out[0, 0:128, 0:128], identf, [bf("identf")], [], "out0")

    S_.emit(final_wait_slots=[n for n in ["out%d" % j for j in range(4)] + ["dbg", "dbgp"] if n in S_.slots])
    return nc, S_


_CACHE = {}


def make_in_map(xs, rel_bias, w_in, b_gates, w_branch_dsa, w_branch_sb, w_out, ln1_g, ln1_b,
                w_ffn_in, conv_w, conv_b, w_ffn_out, ln2_g, ln2_b):
    f = lambda a: np.ascontiguousarray(np.asarray(a, dtype=np.float32))
    m = {
        "x": f(xs),
        "rel_bias": f(rel_bias),
        "w_in": f(w_in[0]),
        "w_branch_dsa": f(w_branch_dsa[0]),
        "w_branch_sb": f(w_branch_sb[0]),
        "w_out": f(w_out[0]),
        "w_ffn_in": f(w_ffn_in[0]),
        "w_ffn_out": f(w_ffn_out[0]),
        "lnp": f(np.stack([ln1_g[0], ln1_b[0], ln2_g[0], ln2_b[0]], 0)),
        "bg_t": f(np.asarray(b_gates[0]).reshape(16, 128).T),
        "cw_t": f(np.asarray(conv_w[0]).reshape(3, NFC, 128).transpose(2, 1, 0)),
        "cb_t": f(np.asarray(conv_b[0]).reshape(NFC, 128).T),
    }
    m.update(_CONSTS)
    return m


def kernel(x, rel_bias, w_in, b_gates, w_branch_dsa, w_branch_sb, w_out, ln1_g, ln1_b,
           w_ffn_in, conv_w, conv_b, w_ffn_out, ln2_g, ln2_b):
    x = np.asarray(x)
    Bt, S, _ = x.shape
    ncores = 8
    nseq = Bt // ncores
    key = (nseq, S)
    if key not in _CACHE:
        _CACHE[key] = build(nseq, S)[0]
    nc = _CACHE[key]
    in_maps = []
    for c in range(ncores):
        in_maps.append(make_in_map(x[c * nseq:(c + 1) * nseq], rel_bias, w_in, b_gates, w_branch_dsa, w_branch_sb,
                                   w_out, ln1_g, ln1_b, w_ffn_in, conv_w, conv_b, w_ffn_out, ln2_g, ln2_b))
    res = run_bass_kernel_spmd(nc, in_maps, core_ids=list(range(ncores)))
    return np.concatenate([np.asarray(r["out"]) for r in res.results], axis=0).astype(np.float32)
```

```python
import math
import contextlib
import numpy as np
import concourse.bass as bass
import concourse.mybir as mybir
from concourse.bass_utils import run_bass_kernel_spmd

F32 = mybir.dt.float32
BF16 = mybir.dt.bfloat16
AF = mybir.ActivationFunctionType
ALU = mybir.AluOpType
AX = mybir.AxisListType

D = 1024
NH = 8
HD = 64
DFF = 2816
NFC = DFF // 128
NIN = 5704
O_QA, O_KA, O_VA, O_QI, O_KI, O_WI, O_QS, O_KS, O_VS, O_GA, O_GB = (
    0, 512, 1024, 1536, 2048, 2112, 2120, 2632, 3144, 3656, 4680)
LN_EPS = 1e-5
ALPHA = 2.0 ** 0.25
NIT = 24
NEG = -1.0e30

ENGS = ["pe", "act", "dve", "pool", "sp"]


class Buf:
    __slots__ = ("name", "last_w", "readers", "psum")

    def __init__(self, name, psum=False):
        self.name = name
        self.last_w = None
        self.readers = []
        self.psum = psum


class Op:
    __slots__ = ("eng", "fn", "deps", "prio", "cidx", "slot", "dma_val", "ms", "is_target")


class Sched:
    def __init__(self, nc):
        self.nc = nc
        self.ops = []
        self.slots = {}
        self.prio_off = 0.0

    def op(self, eng, fn, reads=(), writes=(), slot=None, prio=None):
        o = Op()
        o.eng = eng
        o.fn = fn
        o.cidx = len(self.ops)
        o.prio = float(o.cidx) + self.prio_off if prio is None else prio
        o.slot = slot
        o.dma_val = None
        o.ms = None
        o.is_target = False
        deps = set()
        for b in reads:
            if b.last_w is not None:
                deps.add(b.last_w)
            if b.psum:
                for r in b.readers:
                    if r.eng != eng:
                        deps.add(r)
        for b in writes:
            if b.last_w is not None:
                deps.add(b.last_w)
            for r in b.readers:
                deps.add(r)
        for b in reads:
            b.readers.append(o)
        for b in writes:
            b.last_w = o
            b.readers = []
        deps.discard(o)
        o.deps = deps
        if slot is not None:
            s = self.slots.setdefault(slot, [None, 0])
            s[1] += 16
            o.dma_val = s[1]
        self.ops.append(o)
        return o

    def emit(self, final_wait_slots=()):
        nc = self.nc
        order = sorted(self.ops, key=lambda o: (o.prio, o.cidx))
        pos = {o: i for i, o in enumerate(order)}
        for o in order:
            for d in o.deps:
                assert pos[d] < pos[o], "priority order violates a dependency"
        queues = {e: [] for e in ENGS}
        for o in order:
            queues[o.eng].append(o)
        for o in order:
            for d in o.deps:
                if d.slot is None:
                    if d.eng == "pe" and o.eng == "pe":
                        continue
                    d.is_target = True
        for e in ENGS:
            n = 0
            for o in queues[e]:
                if o.slot is None and o.is_target:
                    n += 1
                    o.ms = n
        with contextlib.ExitStack() as es:
            esem = {e: es.enter_context(nc.semaphore("s_" + e)) for e in ENGS}
            for name, s in self.slots.items():
                s[0] = es.enter_context(nc.semaphore("d_" + name))
            block = es.enter_context(nc.Block())
            slots = self.slots

            def run(engname, eng):
                waited = {}
                for o in queues[engname]:
                    need = {}
                    for d in o.deps:
                        if d.slot is not None:
                            key = ("d", d.slot)
                            val = d.dma_val
                            sem = slots[d.slot][0]
                        else:
                            if d.eng == "pe" and engname == "pe":
                                continue
                            key = ("e", d.eng)
                            val = d.ms
                            sem = esem[d.eng]
                        if val > need.get(key, (None, 0))[1]:
                            need[key] = (sem, val)
                    for key, (sem, val) in need.items():
                        if waited.get(key, 0) >= val:
                            continue
                        eng.wait_ge(sem, val)
                        waited[key] = val
                    ins = o.fn(eng)
                    if o.slot is not None:
                        ins.then_inc(slots[o.slot][0], 16)
                    elif o.is_target:
                        ins.then_inc(esem[engname], 1)
                if engname == "sp":
                    for name in final_wait_slots:
                        s = slots[name]
                        eng.wait_ge(s[0], s[1])

            block.tensor(lambda eng: run("pe", eng))
            block.scalar(lambda eng: run("act", eng))
            block.vector(lambda eng: run("dve", eng))
            block.gpsimd(lambda eng: run("pool", eng))
            block.sync(lambda eng: run("sp", eng))
        self.stats = {e: len(queues[e]) for e in ENGS}


def _t5_bucket(rel):
    nb = 16
    max_exact = 8
    side = np.where(rel > 0, nb, 0)
    n = np.abs(rel)
    nf = np.maximum(n, 1).astype(np.float32)
    large = max_exact + (np.log(nf / max_exact) / math.log(128 / max_exact) * (nb - max_exact)).astype(np.int32)
    large = np.minimum(large, nb - 1)
    return side + np.where(n < max_exact, n, large)


def host_constants():
    s = np.arange(128)[:, None]
    t = np.arange(128)[None, :]
    c = {}
    c["c_ident"] = np.eye(128, dtype=np.float32)
    c["c_trineg"] = np.where(s >= t, -8.0, 0.0).astype(np.float32)
    c["c_causal"] = (s < t).astype(np.float32)
    c["c_dsaneg"] = np.where((np.arange(128)[None, :] // 64) > (np.arange(128)[:, None] // 64), NEG, 0.0).astype(np.float32)
    oh = []
    ohidx = []
    for kind in range(2):
        rel = (s - t) - 128 * kind
        bk = _t5_bucket(rel)
        for b in sorted(set(bk.flatten().tolist())):
            oh.append((bk == b).astype(np.float32))
            ohidx.append((kind, int(b)))
    c["c_oh"] = np.stack(oh, 0).transpose(1, 0, 2).copy()
    c["c_pow2"] = np.tile((2.0 ** -(np.arange(NIT + 1) + 1.0)).astype(np.float32)[None, :], (128, 1))
    return c, ohidx


_CONSTS, _OHIDX = host_constants()
N_OH = len(_OHIDX)


def build(nseq, S, debug=False, stop_after=99):
    assert S % 512 == 0
    NB = S // 128
    NSB = S // 512
    NSEL = min(256, S // 4)
    nc = bass.Bass("TRN2", target_bir_lowering=False)
    S_ = Sched(nc)
    op = S_.op

    def dram_in(name, shape, dt=F32):
        return nc.dram_tensor(name, list(shape), dt, kind="ExternalInput").ap()

    x = dram_in("x", [nseq, S, D])
    rel_bias = dram_in("rel_bias", [32, 8])
    w_in = dram_in("w_in", [D, NIN])
    w_bd = dram_in("w_branch_dsa", [512, D])
    w_bs = dram_in("w_branch_sb", [512, D])
    w_out = dram_in("w_out", [D, D])
    w_fi = dram_in("w_ffn_in", [D, 2 * DFF])
    w_fo = dram_in("w_ffn_out", [DFF, D])
    lnp = dram_in("lnp", [4, D])
    bg_d = dram_in("bg_t", [128, 16])
    cw_d = dram_in("cw_t", [128, NFC, 3])
    cb_d = dram_in("cb_t", [128, NFC])
    cd = {k: dram_in(k, v.shape) for k, v in _CONSTS.items()}
    out = nc.dram_tensor("out", [nseq, S, D], F32, kind="ExternalOutput").ap()
    dbg = {}

    def dbg_out(name, shape):
        dbg[name] = nc.dram_tensor(name, list(shape), F32, kind="ExternalOutput").ap()
        return dbg[name]

    SB_BASE = 16512 + 2048
    cursor = [SB_BASE]

    def alloc(name, shape, dt, at=None):
        nbytes = int(np.prod(shape[1:])) * (2 if dt == BF16 else 4)
        nbytes = (nbytes + 63) // 64 * 64
        if at is None:
            off = cursor[0]
            cursor[0] += nbytes
        else:
            off = at[0]
            at[0] += nbytes
        return nc.alloc_sbuf_tensor_at(name, list(shape), dt, offset=off, align_bytes=64).ap()

    ident = alloc("ident", [128, 128], BF16)
    identf = alloc("identf", [128, 128], F32)
    trineg = alloc("trineg", [128, 128], BF16)
    onesneg = alloc("onesneg", [128, 128], BF16)
    causal = alloc("causal", [128, 128], BF16)
    dsaneg = alloc("dsaneg", [128, 128], F32)
    zeros = alloc("zeros", [128, 512], BF16)
    bias8T = alloc("bias8T", [128, 16, 128], BF16)
    rbb = alloc("rbb", [128, 256], F32)
    pow2 = alloc("pow2", [128, NIT + 1], F32)
    bg = alloc("bg", [128, 16], F32)
    cw = alloc("cw", [128, NFC, 3], F32)
    cbv = alloc("cbv", [128, NFC], F32)
    small = alloc("small", [128, 64], F32)
    y_aT = alloc("y_aT", [128, 4, S], BF16)
    qiT = alloc("qiT", [128, 4, S], BF16)
    y_sT = qiT
    wbuf = [alloc("wbuf%d" % i, [128, 8, 512], BF16) for i in range(2)]
    xtok = [alloc("xtok%d" % i, [128, 1024], BF16) for i in range(2)]
    region0 = cursor[0]
    ra = [region0]
    xT_off = [ra[0]]
    xT = alloc("xT", [128, 8, S], BF16, ra)
    oh = alloc("oh", [128, N_OH, 128], BF16, xT_off)
    bacc = alloc("bacc", [128, 128], F32, xT_off)
    qT = alloc("qT", [128, 4, S], BF16, ra)
    kT = alloc("kT", [128, 4, S], BF16, ra)
    kiT = alloc("kiT", [128, S], BF16, ra)
    vaug = alloc("vaug", [128, NB, NH, 66], BF16, ra)
    wi = alloc("wi", [128, NB, 8], F32, ra)
    wki = alloc("wki", [128, 8, 128], BF16, ra)
    wwi = alloc("wwi", [128, 8, 8], BF16, ra)
    sc_off = [ra[0]]
    sc = alloc("sc", [128, S], F32, ra)
    rl = [alloc("rl%d" % i, [128, 512], F32, ra) for i in range(2)]
    mask = alloc("mask", [128, S], BF16, ra)
    junk = mask
    maskT = alloc("maskT", [128, NB, 512], BF16, ra)
    ex = [alloc("ex%d" % i, [128, 512], BF16, ra) for i in range(2)]
    e1 = [alloc("e1_%d" % i, [128, 512], F32, sc_off) for i in range(2)]
    sp_ = [alloc("sp%d" % i, [128, 512], BF16, sc_off) for i in range(2)]
    PT = [alloc("PT%d" % i, [128, 512], BF16, ra) for i in range(2)]
    Rbs = [alloc("Rb%d" % i, [128, 512], BF16, sc_off) for i in range(2)]
    assert sc_off[0] <= ra[0]
    ytok = alloc("ytok", [128, 4, 512], BF16, ra)
    bis = alloc("bis", [128, 8 + NIT + 1], F32, ra)
    rec = alloc("rec", [128, 4], F32, ra)
    rf = [region0]
    lnt = alloc("lnt", [128, 4, D], F32, rf)
    xr = alloc("xr", [128, 4, D], F32, rf)
    xTt = alloc("xTt", [128, 8, 512], BF16, rf)
    gtmp = [alloc("gtmp%d" % i, [128, 512], BF16, rf) for i in range(2)]
    mtmp = alloc("mtmp", [128, 512], F32, rf)
    mT = alloc("mT", [128, 8, 512], BF16, rf)
    x1T = xTt
    wbr = alloc("wbr", [128, 4, D], BF16, rf)
    hT = alloc("hT", [128, NFC, 512], BF16, rf)
    wfo = alloc("wfo", [128, NFC, D], BF16, rf)
    halo = alloc("halo", [128, NFC, 2], F32, rf)
    cbuf = [alloc("cbuf%d" % i, [128, 512], F32, rf) for i in range(2)]
    t2 = [alloc("t2_%d" % i, [128, 512], F32, rf) for i in range(2)]
    lnsm = alloc("lnsm", [128, 32], F32, rf)
    sb_limit = nc.SBUF_PARTITION_SIZE_BYTES
    print("SBUF bytes: common", region0, "att", ra[0], "ffn", rf[0])
    assert max(ra[0], rf[0]) <= nc.SBUF_PARTITION_SIZE_BYTES, (ra[0], rf[0])

    banks = [nc.alloc_psum_tensor("bank%d" % i, [128, 512], F32).ap() for i in range(6)]
    banksbf = {i: nc.alloc_psum_tensor("bankbf%d" % i, [128, 8, 128], BF16).ap() for i in (6, 7)}
    BK = [Buf("bank%d" % i, psum=True) for i in range(8)]

    B = {}
    last_barrier = [None]

    def bf(name):
        if name not in B:
            B[name] = Buf(name)
            B[name].last_w = last_barrier[0]
        return B[name]

    def mm(o, l, r, start, stop, reads, writes, **kw):
        return op("pe", lambda e: e.matmul(o, lhsT=l, rhs=r, start=start, stop=stop, **kw), reads, writes)

    def tr(o, i, idn, reads, writes):
        return op("pe", lambda e: e.transpose(o, i, idn), reads, writes)

    def act(o, i, func, reads, writes, **kw):
        return op("act", lambda e: e.activation(o, i, func, **kw), reads, writes)

    def dma(eng, o, i, reads, writes, slot):
        return op(eng, lambda e: e.dma_start(out=o, in_=i), reads, writes, slot=slot)

    def ts(eng, o, i, s1, s2, op0, op1=None, reads=(), writes=(), accum=None):
        if op1 is None:
            return op(eng, lambda e: e.tensor_scalar(o, i, s1, None, op0), reads, writes)
        if accum is None:
            return op(eng, lambda e: e.tensor_scalar(o, i, s1, s2, op0, op1), reads, writes)
        return op(eng, lambda e: e.tensor_scalar(o, i, s1, s2, op0, op1, accum_out=accum), reads, writes)

    def stt(o, i0, sc_, i1, op0, op1, reads, writes):
        return op("dve", lambda e: e.scalar_tensor_tensor(o, i0, sc_, i1, op0, op1), reads, writes)

    def tt(eng, o, i0, i1, aop, reads, writes):
        return op(eng, lambda e: e.tensor_tensor(o, i0, i1, aop), reads, writes)

    def cp(eng, o, i, reads, writes):
        if eng == "act":
            return op("act", lambda e: e.copy(o, i), reads, writes)
        return op(eng, lambda e: e.tensor_copy(o, i), reads, writes)

    evq = [0]

    def evac(o, i, reads, writes):
        evq[0] ^= 1
        return cp("act" if evq[0] else "dve", o, i, reads, writes)

    def setup():
        dma("pool", ident, cd["c_ident"], [], [bf("ident")], "c0")
        dma("sp", identf, cd["c_ident"], [], [bf("identf")], "c1")
        dma("pool", trineg, cd["c_trineg"], [], [bf("trineg")], "c2")
        dma("pool", causal, cd["c_causal"], [], [bf("causal")], "c3")
        dma("sp", dsaneg, cd["c_dsaneg"], [], [bf("dsaneg")], "c4")
        dma("sp", pow2, cd["c_pow2"], [], [bf("pow2")], "c5")
        dma("sp", bg, bg_d, [], [bf("bg")], "c6")
        dma("sp", cw, cw_d, [], [bf("cw")], "c7")
        dma("sp", cbv, cb_d, [], [bf("cbv")], "c8")
        dma("sp", rbb, rel_bias.rearrange("a b -> (a b)").partition_broadcast(128), [], [bf("rbb")], "c9")
        dma("pool", oh, cd["c_oh"], [], [bf("oh")], "c10")
        op("dve", lambda e: e.memset(onesneg, -8.0), [], [bf("onesneg")])
        op("dve", lambda e: e.memset(zeros, 0.0), [], [bf("zeros")])
        op("pool", lambda e: e.memset(vaug, 1.0), [], [bf("vaug_all")])
        for h in range(NH):
            for kind in range(2):
                idxs = [i for i, (k, b) in enumerate(_OHIDX) if k == kind]
                first = True
                for i in idxs:
                    b = _OHIDX[i][1]
                    scal = rbb[:, b * 8 + h:b * 8 + h + 1]
                    if first:
                        ts("dve", bacc, oh[:, i, :], scal, None, ALU.mult, reads=[bf("oh"), bf("rbb")], writes=[bf("bacc")])
                        first = False
                    else:
                        stt(bacc, oh[:, i, :], scal, bacc, ALU.mult, ALU.add, [bf("oh"), bf("rbb"), bf("bacc")], [bf("bacc")])
                ts("dve", bias8T[:, h * 2 + kind, :], bacc, rbb[:, 120 + h:121 + h], 8.0, ALU.subtract, ALU.mult,
                   reads=[bf("bacc"), bf("rbb")], writes=[bf("bias8T")])
        op("dve", lambda e: e.memset(small[:, 0:1], 0.0), [bf("oh"), bf("bacc"), bf("bias8T")], [])

    wslot = [0]

    def load_w(src_ap, ncols, kchunks=8):
        i = wslot[0]
        wslot[0] ^= 1
        dst = wbuf[i].rearrange("p c n -> p (c n)")[:, 0:kchunks * ncols].rearrange("p (c n) -> p c n", c=kchunks)
        dma("pool", dst, src_ap.rearrange("(c p) n -> p c n", p=128), [], [bf("wbuf%d" % i)], "w%d" % i)
        return dst, bf("wbuf%d" % i)

    bankrr = [0]

    def next_bank(lo=0, hi=4):
        b = lo + bankrr[0] % (hi - lo)
        bankrr[0] += 1
        return b

    def phase_x(b):
        for tb in range(NB):
            i = tb % 2
            dma("pool", xtok[i], x[b, tb * 128:(tb + 1) * 128, :], [], [bf("xtok%d" % i)], "xtok%d" % i)
            bk = 6 + i
            for cc in range(8):
                tr(banksbf[bk][:, cc, :], xtok[i][:, cc * 128:(cc + 1) * 128], ident,
                   [bf("xtok%d" % i), bf("ident")], [BK[bk]])
            evac(xT[:, :, tb * 128:(tb + 1) * 128], banksbf[bk][:, :, :], [BK[bk]], [bf("xT%d" % tb)])

    def proj_feat(wt, wb, ncol_chunks, dst, dstname, col0=0):
        for j in range(ncol_chunks):
            for tt_ in range(NSB):
                bk = next_bank()
                for cc in range(8):
                    mm(banks[bk][:, :], wt[:, cc, col0 + j * 128:col0 + (j + 1) * 128], xT[:, cc, tt_ * 512:(tt_ + 1) * 512],
                       cc == 0, cc == 7, [wb] + [bf("xT%d" % (tt_ * 4 + q)) for q in range(4)], [BK[bk]])
                evac(dst[:, j, tt_ * 512:(tt_ + 1) * 512], banks[bk][:, :], [BK[bk]], [bf("%s_%d_%d" % (dstname, j, tt_))])

    def proj_v(wt, wb):
        for tb in range(NB):
            bk = next_bank()
            for cc in range(8):
                mm(banks[bk][:, :], xT[:, cc, tb * 128:(tb + 1) * 128], wt[:, cc, 0:512], cc == 0, cc == 7,
                   [wb, bf("xT%d" % tb)], [BK[bk]])
            evac(vaug[:, tb, :, 0:64], banks[bk][:, :].rearrange("p (h d) -> p h d", h=NH), [BK[bk], bf("vaug_all")], [bf("vaug%d" % tb)])

    def phase_proj_dsa(b):
        wt, wb = load_w(w_in[:, O_QA:O_QA + 512], 512)
        proj_feat(wt, wb, 4, qT, "qT")
        wt, wb = load_w(w_in[:, O_KA:O_KA + 512], 512)
        proj_feat(wt, wb, 4, kT, "kT")
        wt, wb = load_w(w_in[:, O_VA:O_VA + 512], 512)
        proj_v(wt, wb)
        wt, wb = load_w(w_in[:, O_QI:O_QI + 512], 512)
        proj_feat(wt, wb, 4, qiT, "qiT")
        src = w_in[:, O_KI:O_KI + 64].rearrange("(c p) n -> p c n", p=128)
        dma("pool", wki[:, :, 0:64], src, [], [bf("wki")], "wki")
        dma("pool", wki[:, :, 64:128], src, [], [bf("wki")], "wki")
        dma("pool", wwi, w_in[:, O_WI:O_WI + 8].rearrange("(c p) n -> p c n", p=128), [], [bf("wwi")], "wwi")
        for tt_ in range(NSB):
            bk = next_bank()
            for cc in range(8):
                mm(banks[bk][:, :], wki[:, cc, :], xT[:, cc, tt_ * 512:(tt_ + 1) * 512], cc == 0, cc == 7,
                   [bf("wki")] + [bf("xT%d" % (tt_ * 4 + q)) for q in range(4)], [BK[bk]])
            evac(kiT[:, tt_ * 512:(tt_ + 1) * 512], banks[bk][:, :], [BK[bk]], [bf("kiT_%d" % tt_)])
        for tb in range(NB):
            bk = next_bank()
            for cc in range(8):
                mm(banks[bk][:, 0:8], xT[:, cc, tb * 128:(tb + 1) * 128], wwi[:, cc, :], cc == 0, cc == 7,
                   [bf("wwi"), bf("xT%d" % tb)], [BK[bk]])
            op("act", lambda e, o_=wi[:, tb, :], i_=banks[bk][:, 0:8]: e.mul(o_, i_, float(8 ** -0.5)), [BK[bk]], [bf("wi%d" % tb)])

    def phase_proj_sb(b):
        wt, wb = load_w(w_in[:, O_QS:O_QS + 512], 512)
        proj_feat(wt, wb, 4, qT, "qT")
        wt, wb = load_w(w_in[:, O_KS:O_KS + 512], 512)
        proj_feat(wt, wb, 4, kT, "kT")
        wt, wb = load_w(w_in[:, O_VS:O_VS + 512], 512)
        proj_v(wt, wb)

    def hrows(h):
        r0 = (h % 2) * 64
        return slice(r0, r0 + 64), h // 2

    def dsa_scores(b, tb):
        nk = (tb + 1) * 128
        qs = tb // 4
        scb = bf("sc")
        nkt = (nk + 511) // 512
        for kt in range(nkt):
            k0 = kt * 512
            n = min(512, nk - k0)
            for h in range(NH):
                rs, ch = hrows(h)
                bk = next_bank()
                mm(banks[bk][:, 0:n], qiT[rs, ch, tb * 128:(tb + 1) * 128], kiT[rs, k0:k0 + n], True, True,
                   [bf("qiT_%d_%d" % (ch, qs)), bf("kiT_%d" % kt)], [BK[bk]])
                r = h % 2
                act(rl[r][:, 0:n], banks[bk][:, 0:n], AF.Relu, [BK[bk]], [bf("rl%d" % r)], scale=0.125)
                wsc = wi[:, tb, h:h + 1]
                if h == 0:
                    ts("dve", sc[:, k0:k0 + n], rl[r][:, 0:n], wsc, None, ALU.mult, reads=[bf("rl%d" % r), bf("wi%d" % tb)], writes=[scb])
                else:
                    stt(sc[:, k0:k0 + n], rl[r][:, 0:n], wsc, sc[:, k0:k0 + n], ALU.mult, ALU.add,
                        [bf("rl%d" % r), bf("wi%d" % tb), scb], [scb])
        bb = bf("bis")
        op("dve", lambda e: e.tensor_reduce(bis[:, 0:1], sc[:, 0:nk], AX.X, ALU.max), [scb], [bb])
        op("dve", lambda e: e.tensor_reduce(bis[:, 1:2], sc[:, 0:nk], AX.X, ALU.min), [scb, bb], [bb])
        tt("dve", sc[:, tb * 128:nk], sc[:, tb * 128:nk], dsaneg, ALU.add, [scb, bf("dsaneg")], [scb])
        ts("dve", bis[:, 2:3], bis[:, 0:1], bis[:, 1:2], 1.002, ALU.subtract, ALU.mult, reads=[bb], writes=[bb])
        tt("dve", bis[:, 1:2], bis[:, 0:1], bis[:, 2:3], ALU.subtract, [bb], [bb])
        ts("dve", bis[:, 8:8 + NIT + 1], pow2, bis[:, 2:3], None, ALU.mult, reads=[bb, bf("pow2")], writes=[bb])
        tt("dve", bis[:, 3:4], bis[:, 1:2], bis[:, 8:9], ALU.add, [bb], [bb])
        jb = bf("mask")
        for it in range(NIT):
            ts("dve", junk[:, 0:nk], sc[:, 0:nk], bis[:, 3:4], 0.0, ALU.is_ge, ALU.add, reads=[scb, bb], writes=[jb, bb],
               accum=bis[:, 4:5])
            stt(bis[:, 5:6], bis[:, 4:5], float(NSEL) - 0.5, bis[:, 8 + it:9 + it], ALU.is_ge, ALU.mult, [bb], [bb])
            stt(bis[:, 3:4], bis[:, 5:6], bis[:, 9 + it:10 + it], bis[:, 3:4], ALU.subtract, ALU.add, [bb], [bb])
        tt("dve", bis[:, 6:7], bis[:, 3:4], bis[:, 8 + NIT:9 + NIT], ALU.subtract, [bb], [bb])
        ts("dve", mask[:, 0:nk], sc[:, 0:nk], bis[:, 6:7], None, ALU.is_ge, reads=[scb, bb], writes=[bf("mask")])
        j = tb % 4
        nsb = tb + 1
        for g0 in range(0, nsb, 8):
            g1 = min(nsb, g0 + 8)
            bk = 6 + (g0 // 8 + tb) % 2
            for sb in range(g0, g1):
                tr(banksbf[bk][:, sb - g0, :], mask[:, sb * 128:(sb + 1) * 128], ident, [bf("mask"), bf("ident")], [BK[bk]])
            evac(maskT[:, g0:g1, j * 128:(j + 1) * 128], banksbf[bk][:, 0:g1 - g0, :], [BK[bk]], [bf("maskT%d" % j)])

    def yt_transposes(b, qs, dstT, dstname):
        for j in range(4):
            bk = 6 + j % 2
            for c in range(4):
                tr(banksbf[bk][:, c, :], ytok[:, j, c * 128:(c + 1) * 128], ident, [bf("ytok%d" % j), bf("ident")], [BK[bk]])
            tb = qs * 4 + j
            evac(dstT[:, :, tb * 128:(tb + 1) * 128], banksbf[bk][:, 0:4, :], [BK[bk]], [bf("%s%d" % (dstname, tb))])

    def dsa_attend(b, qs):
        nsb_tot = qs * 4 + 4
        for h in range(NH):
            rs, ch = hrows(h)
            ybk = 4 + h % 2
            mm(banks[ybk][:, 0:260], zeros[:, 0:128], zeros[:, 0:260], True, False, [bf("zeros")], [BK[ybk]])
            for sb in range(nsb_tot):
                j0 = max(0, sb - qs * 4)
                c0 = j0 * 128
                n = 512 - c0
                lbk = next_bank()
                near = [(jj, sb == qs * 4 + jj) for jj in range(j0, 4) if (qs * 4 + jj) - sb in (0, 1)]
                mm(banks[lbk][:, c0:512], kT[rs, ch, sb * 128:(sb + 1) * 128], qT[rs, ch, qs * 512 + c0:qs * 512 + 512],
                   True, len(near) == 0, [bf("kT_%d_%d" % (ch, sb // 4)), bf("qT_%d_%d" % (ch, qs))], [BK[lbk]])
                for idx, (jj, isdiag) in enumerate(near):
                    kind = 0 if isdiag else 1
                    mm(banks[lbk][:, jj * 128:(jj + 1) * 128], ident, bias8T[:, h * 2 + kind, :], False, idx == len(near) - 1,
                       [bf("ident"), bf("bias8T")], [BK[lbk]])
                r = sb % 2
                act(ex[r][:, c0:512], banks[lbk][:, c0:512], AF.Exp, [BK[lbk], bf("rbb")], [bf("ex%d" % r)],
                    scale=0.125, bias=rbb[:, 120 + h:121 + h])
                eng = "dve" if (sb % 2 == 0) else "pool"
                tt(eng, PT[r][:, c0:512], ex[r][:, c0:512], maskT[:, sb, c0:512], ALU.mult,
                   [bf("ex%d" % r)] + [bf("maskT%d" % jj) for jj in range(j0, 4)], [bf("PT%d" % r)])
                for jj in range(j0, 4):
                    if stop_after < 3.5:
                        break
                    last = (sb == nsb_tot - 1) and (jj == 3)
                    mm(banks[ybk][:, jj * 65:(jj + 1) * 65], PT[r][:, jj * 128:(jj + 1) * 128], vaug[:, sb, h, 0:65], False, last,
                       [bf("PT%d" % r), bf("vaug%d" % sb)], [BK[ybk]])
            if stop_after < 3.8:
                continue
            yv = banks[ybk][:, 0:260].rearrange("p (j e) -> p j e", e=65)
            rb_ = bf("rec")
            op("dve", lambda e, yv=yv: e.reciprocal(rec[:, 0:4], yv[:, :, 64]), [BK[ybk]], [rb_])
            for jj in range(4):
                ts("dve", ytok[:, jj, h * 64:(h + 1) * 64], yv[:, jj, 0:64], rec[:, jj:jj + 1], None, ALU.mult,
                   reads=[BK[ybk], rb_], writes=[bf("ytok%d" % jj)])
        if stop_after < 3.8:
            return
        yt_transposes(b, qs, y_aT, "y_aT")

    TLE = "dve"
    SBE = "dve"

    def sb_attend(b, qs):
        nsb_tot = qs * 4 + 4
        for h0 in range(0, NH, 2):
            heads = (h0, h0 + 1)
            Rbufs = [bf("Rb0"), bf("Rb1")]
            for si, h in enumerate(heads):
                ybk = 4 + si
                mm(banks[ybk][:, 0:260], zeros[:, 0:128], zeros[:, 0:260], True, False, [bf("zeros")], [BK[ybk]])
                op(SBE, lambda e, si=si: e.memset(Rbs[si], 0.0), [], [Rbufs[si]])
            for sb in range(nsb_tot - 1, -1, -1):
                j0 = max(0, sb - qs * 4)
                c0 = j0 * 128
                diag = sb >= qs * 4
                first = (sb == nsb_tot - 1)
                zb = [next_bank(), next_bank()]
                for si, h in enumerate(heads):
                    rs, ch = hrows(h)
                    mm(banks[zb[si]][:, c0:512], kT[rs, ch, sb * 128:(sb + 1) * 128], qT[rs, ch, qs * 512 + c0:qs * 512 + 512],
                       True, True, [bf("kT_%d_%d" % (ch, sb // 4)), bf("qT_%d_%d" % (ch, qs))], [BK[zb[si]]])
                for si in range(2):
                    act(e1[si][:, c0:512], banks[zb[si]][:, c0:512], AF.Exp, [BK[zb[si]]], [bf("e1_%d" % si)], scale=0.125)
                for si in range(2):
                    ts("dve", e1[si][:, c0:512], e1[si][:, c0:512], 1.0, None, ALU.add, reads=[bf("e1_%d" % si)], writes=[bf("e1_%d" % si)])
                for si in range(2):
                    act(sp_[si][:, c0:512], e1[si][:, c0:512], AF.Ln, [bf("e1_%d" % si)], [bf("sp%d" % si)])
                if diag:
                    for si in range(2):
                        tt(SBE, sp_[si][:, c0:c0 + 128], sp_[si][:, c0:c0 + 128], causal, ALU.mult,
                           [bf("sp%d" % si), bf("causal")], [bf("sp%d" % si)])
                for si in range(2):
                    mm(banks[zb[si]][:, c0:512], trineg, sp_[si][:, c0:512], False, True, [bf("trineg"), bf("sp%d" % si)], [BK[zb[si]]],
                       skip_group_check=True)
                    if not first:
                        mm(banks[zb[si]][:, c0:512], onesneg, Rbs[si][:, c0:512], False, True, [bf("onesneg"), Rbufs[si]], [BK[zb[si]]],
                           skip_group_check=True)
                for si in range(2):
                    act(PT[si][:, c0:512], banks[zb[si]][:, c0:512], AF.Exp, [BK[zb[si]]], [bf("PT%d" % si)], scale=0.125)
                if diag:
                    for si in range(2):
                        tt("dve", PT[si][:, c0:c0 + 128], PT[si][:, c0:c0 + 128], causal, ALU.mult,
                           [bf("PT%d" % si), bf("causal")], [bf("PT%d" % si)])
                if sb > 0:
                    for si in range(2):
                        tt(SBE, Rbs[si][:, c0:512], Rbs[si][:, c0:512], sp_[si][:, c0:512], ALU.add, [Rbufs[si], bf("sp%d" % si)], [Rbufs[si]])
                for si, h in enumerate(heads):
                    ybk = 4 + si
                    for jj in range(j0, 4):
                        last = (sb == 0) and (jj == 3)
                        mm(banks[ybk][:, jj * 65:(jj + 1) * 65], PT[si][:, jj * 128:(jj + 1) * 128], vaug[:, sb, h, 0:65], False, last,
                           [bf("PT%d" % si), bf("vaug%d" % sb)], [BK[ybk]])
            for si, h in enumerate(heads):
                ybk = 4 + si
                yv = banks[ybk][:, 0:260].rearrange("p (j e) -> p j e", e=65)
                for jj in range(4):
                    cp("act", ytok[:, jj, h * 64:(h + 1) * 64], yv[:, jj, 0:64], [BK[ybk]], [bf("ytok%d" % jj)])
        yt_transposes(b, qs, y_sT, "y_sT")

    def layer_norm(eng_hint, j, gi, dst_bufs, src_reads):
        lb = bf("lnsm")
        xb = bf("xr%d" % j)
        for hh in range(2):
            op("dve", lambda e, hh=hh: e.bn_stats(lnsm[:, hh * 6:(hh + 1) * 6], xr[:, j, hh * 512:(hh + 1) * 512]), [xb], [lb])
        op("dve", lambda e: e.bn_aggr(lnsm[:, 12:14], lnsm[:, 0:12]), [lb], [lb])
        ts("dve", lnsm[:, 14:15], lnsm[:, 13:14], LN_EPS, None, ALU.add, reads=[lb], writes=[lb])
        op("act", lambda e: e.sqrt(lnsm[:, 15:16], lnsm[:, 14:15]), [lb], [lb])
        op("dve", lambda e: e.reciprocal(lnsm[:, 16:17], lnsm[:, 15:16]), [lb], [lb])
        ts("dve", xr[:, j, :], xr[:, j, :], lnsm[:, 12:13], lnsm[:, 16:17], ALU.subtract, ALU.mult, reads=[xb, lb], writes=[xb])
        tt(TLE, xr[:, j, :], xr[:, j, :], lnt[:, gi, :], ALU.mult, [xb, bf("lnt")], [xb])
        tt(TLE, xr[:, j, :], xr[:, j, :], lnt[:, gi + 1, :], ALU.add, [xb, bf("lnt")], [xb])

    def phase_tail(b, tt_):
        t0 = tt_ * 512
        for j in range(4):
            dma("sp", xr[:, j, :], x[b, t0 + j * 128:t0 + (j + 1) * 128, :], [], [bf("xr%d" % j)], "xr%d" % j)
        for j in range(4):
            for g in range(2):
                bk = next_bank()
                for c in range(4):
                    cc = g * 4 + c
                    tr(banks[bk][:, c * 128:(c + 1) * 128], xr[:, j, cc * 128:(cc + 1) * 128], identf, [bf("xr%d" % j), bf("identf")], [BK[bk]])
                evac(xTt[:, g * 4:(g + 1) * 4, j * 128:(j + 1) * 128], banks[bk][:, :].rearrange("p (c t) -> p c t", c=4), [BK[bk]], [bf("xTt")])
        for br in range(2):
            wsrc = w_bd if br == 0 else w_bs
            yT_ = y_aT if br == 0 else y_sT
            yname = "y_aT" if br == 0 else "y_sT"
            goff = O_GA if br == 0 else O_GB
            wbt, wbb = wbr, bf("wbr")
            dma("pool", wbr, wsrc.rearrange("(c p) n -> p c n", p=128), [], [wbb], "wbr")
            for half in range(2):
                wgt, wgb = load_w(w_in[:, goff + half * 512:goff + (half + 1) * 512], 512)
                for jc in range(4):
                    jn = half * 4 + jc
                    gbk = next_bank()
                    for cc in range(8):
                        mm(banks[gbk][:, :], wgt[:, cc, jc * 128:(jc + 1) * 128], xTt[:, cc, :], cc == 0, cc == 7, [wgb, bf("xTt")], [BK[gbk]])
                    r = jn % 2
                    act(gtmp[r], banks[gbk][:, :], AF.Sigmoid, [BK[gbk], bf("bg")], [bf("gtmp%d" % r)], bias=bg[:, br * 8 + jn:br * 8 + jn + 1])
                    bbk = next_bank()
                    for k in range(4):
                        mm(banks[bbk][:, :], wbt[:, k, jn * 128:(jn + 1) * 128],
                           yT_[:, k, t0:t0 + 512], k == 0, k == 3, [wbb] + [bf("%s%d" % (yname, tt_ * 4 + q)) for q in range(4)], [BK[bbk]])
                    if br == 0:
                        tt("dve", mT[:, jn, :], gtmp[r], banks[bbk][:, :], ALU.mult, [bf("gtmp%d" % r), BK[bbk]], [bf("mT%d" % jn)])
                    else:
                        tt("dve", mtmp, gtmp[r], banks[bbk][:, :], ALU.mult, [bf("gtmp%d" % r), BK[bbk]], [bf("mtmp")])
                        tt(TLE, mT[:, jn, :], mT[:, jn, :], mtmp, ALU.add, [bf("mtmp"), bf("mT%d" % jn)], [bf("mT%d" % jn)])
        for nh in range(2):
            wot, wob = load_w(w_out[:, nh * 512:(nh + 1) * 512], 512)
            for j in range(4):
                bk = next_bank()
                for k in range(8):
                    mm(banks[bk][:, :], mT[:, k, j * 128:(j + 1) * 128], wot[:, k, 0:512], k == 0, k == 7, [wob, bf("mT%d" % k)], [BK[bk]])
                stt(xr[:, j, nh * 512:(nh + 1) * 512], xr[:, j, nh * 512:(nh + 1) * 512], ALPHA, banks[bk][:, :], ALU.mult, ALU.add,
                    [bf("xr%d" % j), BK[bk]], [bf("xr%d" % j)])
        for j in range(4):
            layer_norm("dve", j, 0, None, None)
        if debug and tt_ == 0 and b == 0:
            dma("sp", dbg["d_x1"], xr[:, 0, :], [bf("xr0")], [], "dbg")
        for j in range(4):
            for g in range(2):
                bk = next_bank()
                for c in range(4):
                    cc = g * 4 + c
                    tr(banks[bk][:, c * 128:(c + 1) * 128], xr[:, j, cc * 128:(cc + 1) * 128], identf, [bf("xr%d" % j), bf("identf")], [BK[bk]])
                evac(x1T[:, g * 4:(g + 1) * 4, j * 128:(j + 1) * 128], banks[bk][:, :].rearrange("p (c t) -> p c t", c=4), [BK[bk]], [bf("xTt")])
        for q in range(0, NFC, 4):
            q1 = min(NFC, q + 4)
            dma("pool", wfo[:, q:q1, :], w_fo[q * 128:q1 * 128, :].rearrange("(c p) n -> p c n", p=128), [], [bf("wfo%d" % (q // 4))], "wfo%d" % (q // 4))
        for g0 in range(0, NFC, 4):
            g1 = min(NFC, g0 + 4)
            ncol = (g1 - g0) * 128
            wut, wub = load_w(w_fi[:, g0 * 128:g0 * 128 + ncol], ncol)
            wgt, wgb = load_w(w_fi[:, DFF + g0 * 128:DFF + g0 * 128 + ncol], ncol)
            for fc in range(g0, g1):
                lc = fc - g0
                ubk = next_bank()
                for cc in range(8):
                    mm(banks[ubk][:, :], wut[:, cc, lc * 128:(lc + 1) * 128], x1T[:, cc, :], cc == 0, cc == 7, [wub, bf("xTt")], [BK[ubk]])
                gbk = next_bank()
                for cc in range(8):
                    mm(banks[gbk][:, :], wgt[:, cc, lc * 128:(lc + 1) * 128], x1T[:, cc, :], cc == 0, cc == 7, [wgb, bf("xTt")], [BK[gbk]])
                r = fc % 2
                cbf = bf("cbuf%d" % r)
                hb = bf("halo")
                U = banks[ubk]
                act(cbuf[r], U[:, :], AF.Identity, [BK[ubk], bf("cw"), bf("cbv")], [cbf], scale=cw[:, fc, 2:3], bias=cbv[:, fc:fc + 1])
                stt(cbuf[r][:, 1:512], U[:, 0:511], cw[:, fc, 1:2], cbuf[r][:, 1:512], ALU.mult, ALU.add, [BK[ubk], cbf, bf("cw")], [cbf])
                stt(cbuf[r][:, 2:512], U[:, 0:510], cw[:, fc, 0:1], cbuf[r][:, 2:512], ALU.mult, ALU.add, [BK[ubk], cbf, bf("cw")], [cbf])
                if tt_ > 0:
                    stt(cbuf[r][:, 0:1], halo[:, fc, 1:2], cw[:, fc, 1:2], cbuf[r][:, 0:1], ALU.mult, ALU.add, [hb, cbf, bf("cw")], [cbf])
                    stt(cbuf[r][:, 0:2], halo[:, fc, 0:2], cw[:, fc, 0:1], cbuf[r][:, 0:2], ALU.mult, ALU.add, [hb, cbf, bf("cw")], [cbf])
                cp("dve", halo[:, fc, :], U[:, 510:512], [BK[ubk], cbf], [hb])
                op("act", lambda e, r=r: e.activation(t2[r], cbuf[r], AF.Square), [cbf], [bf("t2_%d" % r)])
                ts("dve", t2[r], t2[r], 0.0713548163, 1.5957691216, ALU.mult, ALU.add, reads=[bf("t2_%d" % r)], writes=[bf("t2_%d" % r)])
                tt(TLE, t2[r], t2[r], cbuf[r], ALU.mult, [bf("t2_%d" % r), cbf], [bf("t2_%d" % r)])
                act(t2[r], t2[r], AF.Sigmoid, [bf("t2_%d" % r)], [bf("t2_%d" % r)])
                tt(TLE, t2[r], t2[r], cbuf[r], ALU.mult, [bf("t2_%d" % r), cbf], [bf("t2_%d" % r)])
                tt("dve", hT[:, fc, :], t2[r], banks[gbk][:, :], ALU.mult, [bf("t2_%d" % r), BK[gbk]], [bf("hT%d" % fc)])
        for j in range(4):
            for nh in range(2):
                bk = next_bank()
                for fc in range(NFC):
                    mm(banks[bk][:, :], hT[:, fc, j * 128:(j + 1) * 128], wfo[:, fc, nh * 512:(nh + 1) * 512], fc == 0, fc == NFC - 1,
                       [bf("hT%d" % fc), bf("wfo%d" % (fc // 4))], [BK[bk]])
                stt(xr[:, j, nh * 512:(nh + 1) * 512], xr[:, j, nh * 512:(nh + 1) * 512], ALPHA, banks[bk][:, :], ALU.mult, ALU.add,
                    [bf("xr%d" % j), BK[bk]], [bf("xr%d" % j)])
            layer_norm("dve", j, 2, None, None)
            dma("sp", out[b, t0 + j * 128:t0 + (j + 1) * 128, :], xr[:, j, :], [bf("xr%d" % j)], [], "out%d" % j)

    def barrier():
        o = op("dve", lambda e: e.memset(small[:, 1:2], 0.0), [], list(B.values()) + BK)
        last_barrier[0] = o

    if debug:
        dbg_out("d_x1", [128, D])
        dbg_out("d_yaT", [128, 4, S])
        dbg_out("d_ysT", [128, 4, S])

    setup()
    for b in range(nseq):
        if stop_after < 1:
            break
        barrier()
        op("pool", lambda e: e.memset(vaug[:, :, :, 64:66], 1.0), [], [bf("vaug_all")])
        phase_x(b)
        if stop_after < 2:
            break
        phase_proj_dsa(b)
        if stop_after < 3:
            break
        for qs in range(NSB):
            for tb in range(qs * 4, qs * 4 + 4):
                dsa_scores(b, tb)
            if stop_after > 3:
                dsa_attend(b, qs)
        if stop_after < 4.4:
            break
        barrier()
        phase_proj_sb(b)
        for qs in range(NSB):
            if stop_after > 4.7:
                sb_attend(b, qs)
        if debug and b == 0:
            dma("pool", dbg["d_yaT"], y_aT, [bf("y_aT%d" % q) for q in range(NB)], [], "dbgp")
            dma("pool", dbg["d_ysT"], y_sT, [bf("y_sT%d" % q) for q in range(NB)], [], "dbgp")
        if stop_after < 6:
            break
        barrier()
        dma("sp", lnt, lnp.partition_broadcast(128), [], [bf("lnt")], "lnt")
        for tt_ in range(NSB):
            phase_tail(b, tt_)
    if stop_after < 6:
        barrier()
        dma("sp", out[0, 0:128, 0:128], identf, [bf("identf")], [], "out0")

    S_.emit(final_wait_slots=[n for n in ["out%d" % j for j in range(4)] + ["dbg", "dbgp"] if n in S_.slots])
    return nc, S_


_CACHE = {}


def make_in_map(xs, rel_bias, w_in, b_gates, w_branch_dsa, w_branch_sb, w_out, ln1_g, ln1_b,
                w_ffn_in, conv_w, conv_b, w_ffn_out, ln2_g, ln2_b):
    f = lambda a: np.ascontiguousarray(np.asarray(a, dtype=np.float32))
    m = {
        "x": f(xs),
        "rel_bias": f(rel_bias),
        "w_in": f(w_in[0]),
        "w_branch_dsa": f(w_branch_dsa[0]),
        "w_branch_sb": f(w_branch_sb[0]),
        "w_out": f(w_out[0]),
        "w_ffn_in": f(w_ffn_in[0]),
        "w_ffn_out": f(w_ffn_out[0]),
        "lnp": f(np.stack([ln1_g[0], ln1_b[0], ln2_g[0], ln2_b[0]], 0)),
        "bg_t": f(np.asarray(b_gates[0]).reshape(16, 128).T),
        "cw_t": f(np.asarray(conv_w[0]).reshape(3, NFC, 128).transpose(2, 1, 0)),
        "cb_t": f(np.asarray(conv_b[0]).reshape(NFC, 128).T),
    }
    m.update(_CONSTS)
    return m


def kernel(x, rel_bias, w_in, b_gates, w_branch_dsa, w_branch_sb, w_out, ln1_g, ln1_b,
           w_ffn_in, conv_w, conv_b, w_ffn_out, ln2_g, ln2_b):
    x = np.asarray(x)
    Bt, S, _ = x.shape
    ncores = 8
    nseq = Bt // ncores
    key = (nseq, S)
    if key not in _CACHE:
        _CACHE[key] = build(nseq, S)[0]
    nc = _CACHE[key]
    in_maps = []
    for c in range(ncores):
        in_maps.append(make_in_map(x[c * nseq:(c + 1) * nseq], rel_bias, w_in, b_gates, w_branch_dsa, w_branch_sb,
                                   w_out, ln1_g, ln1_b, w_ffn_in, conv_w, conv_b, w_ffn_out, ln2_g, ln2_b))
    res = run_bass_kernel_spmd(nc, in_maps, core_ids=list(range(ncores)))
    return np.concatenate([np.asarray(r["out"]) for r in res.results], axis=0).astype(np.float32)
```
